# Optimizing a Trainium2 kernel written in Bass

```python
import math
import jax, jax.numpy as jnp
from jax import lax
import numpy as np

D_MODEL = 2048
BATCH = 8
SEQ = 2048
DEPTH = 1

PLE_DIM = 256
EPS = 1e-6
SSM_EXPAND = 2
D_INNER = SSM_EXPAND * D_MODEL
SSM_HEAD_DIM = 64
SSM_HEADS = D_INNER // SSM_HEAD_DIM
SSM_GROUPS = 8
SSM_STATE = 128
SSM_CONV = 4
SSM_CHUNK = 128
SSM_BC = SSM_GROUPS * SSM_STATE
SSM_CONV_DIM = D_INNER + 2 * SSM_BC
ATTN_HEADS = 16
ATTN_HEAD_DIM = 128
ATTN_KV_GROUPS = 4
ATTN_WIDTH = ATTN_HEADS * ATTN_HEAD_DIM
KV_WIDTH = ATTN_KV_GROUPS * ATTN_HEAD_DIM
CMP_BLOCK = 32
CMP_STRIDE = 16
SEL_BLOCK = 64
N_SEL = 16
WINDOW = 512
Q_BLOCK = 128
SEL_Q_BLOCK = 16
ROPE_THETA = 500000.0
ROPE_DIM = ATTN_HEAD_DIM // 4
D_FF = 5632
FFN_CONV = 3
IN_SIZES = (D_INNER, SSM_CONV_DIM, SSM_HEADS,
            ATTN_WIDTH, KV_WIDTH, KV_WIDTH, KV_WIDTH, KV_WIDTH, KV_WIDTH, KV_WIDTH,
            3 * ATTN_HEADS, D_MODEL, D_MODEL)
N_IN = sum(IN_SIZES)
NEG_INF = -1e30
FORCE_SCORE = 1e4

kernel_name = "hybrid_ssd_nsa_gated_merge_layer"


def rmsnorm(x, w):
    xf = x.astype(jnp.float32)
    y = xf * lax.rsqrt(jnp.mean(xf * xf, axis=-1, keepdims=True) + EPS)
    return (y * w.astype(jnp.float32)).astype(x.dtype)


def causal_dwconv(x, w, b):
    width = w.shape[0]
    xp = jnp.pad(x, ((0, 0), (width - 1, 0), (0, 0)))
    y = lax.conv_general_dilated(xp, w[:, None, :], window_strides=(1,), padding='VALID',
                                 dimension_numbers=('NWC', 'WIO', 'NWC'),
                                 feature_group_count=x.shape[-1])
    return y + b


def rope_partial(x, pos):
    half = ROPE_DIM // 2
    inv = ROPE_THETA ** (-(jnp.arange(half, dtype=jnp.float32) * 2.0 / ROPE_DIM))
    ang = pos.astype(jnp.float32)[:, None] * inv[None, :]
    cos = jnp.cos(ang)[None, :, None, :]
    sin = jnp.sin(ang)[None, :, None, :]
    xf = x.astype(jnp.float32)
    x1 = xf[..., :half]
    x2 = xf[..., half:ROPE_DIM]
    out = jnp.concatenate([x1 * cos - x2 * sin, x2 * cos + x1 * sin, xf[..., ROPE_DIM:]], axis=-1)
    return out.astype(x.dtype)


def ssd_chunked(xh, dt, a, bm, cm):
    f32 = jnp.float32
    Bsz, S, H, P = xh.shape
    G, N = bm.shape[-2:]
    R = H // G
    L = SSM_CHUNK
    nc = S // L
    x = (xh.astype(f32) * dt[..., None]).reshape(Bsz, nc, L, G, R, P)
    adt = (dt * a).reshape(Bsz, nc, L, G, R)
    bm = bm.astype(f32).reshape(Bsz, nc, L, G, N)
    cm = cm.astype(f32).reshape(Bsz, nc, L, G, N)
    acum = jnp.moveaxis(jnp.cumsum(adt, axis=2), 2, -1)
    tril = jnp.tril(jnp.ones((L, L), dtype=bool))
    seg = acum[..., :, None] - acum[..., None, :]
    decay = jnp.exp(jnp.where(tril, seg, -jnp.inf))
    cb = jnp.einsum('bclgn,bcsgn->bcgls', cm, bm)
    y_diag = jnp.einsum('bcgls,bcgrls,bcsgrp->bclgrp', cb, decay, x)
    decay_states = jnp.exp(acum[..., -1:] - acum)
    states = jnp.einsum('bclgn,bcgrl,bclgrp->bcgrpn', bm, decay_states, x)
    chunk_decay = jnp.exp(acum[..., -1])

    def step(h, inp):
        st, dec = inp
        return h * dec[..., None, None] + st, h

    h0 = jnp.zeros((Bsz, G, R, P, N), f32)
    _, prev = lax.scan(step, h0, (jnp.moveaxis(states, 1, 0), jnp.moveaxis(chunk_decay, 1, 0)))
    prev = jnp.moveaxis(prev, 0, 1)
    y_off = jnp.einsum('bclgn,bcgrpn,bcgrl->bclgrp', cm, prev, jnp.exp(acum))
    return (y_diag + y_off).reshape(Bsz, S, H, P)


def mamba2_mixer(z, xbc, dt_raw, conv_w, conv_b, dt_bias, a_log, d_skip, norm_w):
    f32 = jnp.float32
    Bsz, S, _ = z.shape
    xbc = jax.nn.silu(causal_dwconv(xbc, conv_w, conv_b))
    xs = xbc[..., :D_INNER].reshape(Bsz, S, SSM_HEADS, SSM_HEAD_DIM)
    bm = xbc[..., D_INNER:D_INNER + SSM_BC].reshape(Bsz, S, SSM_GROUPS, SSM_STATE)
    cm = xbc[..., D_INNER + SSM_BC:].reshape(Bsz, S, SSM_GROUPS, SSM_STATE)
    dt = jax.nn.softplus(dt_raw.astype(f32) + dt_bias.astype(f32))
    a = -jnp.exp(a_log.astype(f32))
    y = ssd_chunked(xs, dt, a, bm, cm) + d_skip.astype(f32)[:, None] * xs.astype(f32)
    y = y.reshape(Bsz, S, D_INNER) * jax.nn.silu(z.astype(f32))
    yg = y.reshape(Bsz, S, SSM_GROUPS, D_INNER // SSM_GROUPS)
    yg = yg * lax.rsqrt(jnp.mean(yg * yg, axis=-1, keepdims=True) + EPS)
    y = yg.reshape(Bsz, S, D_INNER) * norm_w.astype(f32)
    return y.astype(z.dtype)


def compress_tokens(t, pe, w1, w2):
    Bsz, S, G, d = t.shape
    n_sub = CMP_BLOCK // CMP_STRIDE
    sub = t.reshape(Bsz, S // CMP_STRIDE, CMP_STRIDE, G, d)
    nc = S // CMP_STRIDE - n_sub + 1
    blocks = jnp.concatenate([sub[:, j:j + nc] for j in range(n_sub)], axis=2)
    blocks = blocks + pe[None, None, :, None, :]
    flat = jnp.moveaxis(blocks, 3, 2).reshape(Bsz, nc, G, CMP_BLOCK * d)
    return jax.nn.silu(flat @ w1) @ w2


def cmp_to_sel_map(nc, ns):
    i = np.arange(nc)[:, None]
    j = np.arange(ns)[None, :]
    c_start = i * CMP_STRIDE
    s_start = j * SEL_BLOCK
    m = (c_start < s_start + SEL_BLOCK) & (c_start + CMP_BLOCK > s_start)
    return jnp.asarray(m.astype(np.float32))


def nsa_mixer(q, k_c, v_c, k_s, v_s, k_w, v_w, g_branch, pe_k, pe_v, wk1, wk2, wv1, wv2):
    f32 = jnp.float32
    Bsz, S, _ = q.shape
    G, d = ATTN_KV_GROUPS, ATTN_HEAD_DIM
    R = ATTN_HEADS // G
    scale = 1.0 / math.sqrt(d)
    pos = jnp.arange(S)
    qf = rope_partial(q.reshape(Bsz, S, ATTN_HEADS, d), pos).astype(f32) * scale
    qf = qf.reshape(Bsz, S, G, R, d)
    kvs = lambda t: t.reshape(Bsz, S, G, d)
    k_c = rope_partial(kvs(k_c), pos)
    k_s = rope_partial(kvs(k_s), pos).astype(f32)
    k_w = rope_partial(kvs(k_w), pos).astype(f32)
    v_s = kvs(v_s).astype(f32)
    v_w = kvs(v_w).astype(f32)

    kc = compress_tokens(k_c, pe_k, wk1, wk2).astype(f32)
    vc = compress_tokens(kvs(v_c), pe_v, wv1, wv2).astype(f32)
    nc = kc.shape[1]
    sc = jnp.einsum('bsgrd,bngd->bsgrn', qf, kc)
    cmask = (jnp.arange(nc) * CMP_STRIDE + CMP_BLOCK - 1)[None, :] <= pos[:, None]
    cmask = cmask[None, :, None, None, :]
    p_cmp = jax.nn.softmax(jnp.where(cmask, sc, NEG_INF), axis=-1) * cmask
    o_cmp = jnp.einsum('bsgrn,bngd->bsgrd', p_cmp, vc)

    ns = S // SEL_BLOCK
    imp = jnp.einsum('bsgrn,nj->bsgj', p_cmp, cmp_to_sel_map(nc, ns))
    blk = jnp.arange(ns)[None, :]
    cur = (pos // SEL_BLOCK)[:, None]
    causal_blk = (blk <= cur)[None, :, None, :]
    forced = ((blk == 0) | (blk == cur) | (blk == cur - 1))[None, :, None, :]
    score = jnp.where(forced, FORCE_SCORE, jnp.where(causal_blk, imp, NEG_INF))
    n_top = min(N_SEL, ns)
    _, sel_idx = lax.top_k(score, n_top)
    kb = k_s.reshape(Bsz, ns, SEL_BLOCK, G, d).transpose(0, 3, 1, 2, 4)
    vb = v_s.reshape(Bsz, ns, SEL_BLOCK, G, d).transpose(0, 3, 1, 2, 4)
    bi = jnp.arange(Bsz)[:, None, None, None]
    gi = jnp.arange(G)[None, None, :, None]
    off = jnp.arange(SEL_BLOCK)

    def sel_block(args):
        qb, ib, tb = args
        kg = kb[bi, gi, ib]
        vg = vb[bi, gi, ib]
        s = jnp.einsum('bqgrd,bqgkld->bqgrkl', qb, kg)
        kpos = ib[..., None] * SEL_BLOCK + off
        m = (kpos <= tb[None, :, None, None, None])[:, :, :, None]
        s = jnp.where(m, s, NEG_INF).reshape(Bsz, SEL_Q_BLOCK, G, R, n_top * SEL_BLOCK)
        pr = jax.nn.softmax(s, axis=-1).reshape(Bsz, SEL_Q_BLOCK, G, R, n_top, SEL_BLOCK)
        return jnp.einsum('bqgrkl,bqgkld->bqgrd', pr, vg)

    nqs = S // SEL_Q_BLOCK
    qs = qf.reshape(Bsz, nqs, SEL_Q_BLOCK, G, R, d).swapaxes(0, 1)
    isb = sel_idx.reshape(Bsz, nqs, SEL_Q_BLOCK, G, n_top).swapaxes(0, 1)
    tsb = pos.reshape(nqs, SEL_Q_BLOCK)
    o_sel = lax.map(sel_block, (qs, isb, tsb)).swapaxes(0, 1).reshape(Bsz, S, G, R, d)

    kwp = jnp.pad(k_w, ((0, 0), (WINDOW, 0), (0, 0), (0, 0)))
    vwp = jnp.pad(v_w, ((0, 0), (WINDOW, 0), (0, 0), (0, 0)))
    span = WINDOW + Q_BLOCK
    qi = jnp.arange(Q_BLOCK)[:, None]
    kj = jnp.arange(span)[None, :]
    diff = qi + WINDOW - kj

    def win_block(args):
        qb, n = args
        start = n * Q_BLOCK
        kk = lax.dynamic_slice_in_dim(kwp, start, span, axis=1)
        vv = lax.dynamic_slice_in_dim(vwp, start, span, axis=1)
        s = jnp.einsum('bqgrd,bkgd->bqgrk', qb, kk)
        m = (diff >= 0) & (diff < WINDOW) & (start - WINDOW + kj >= 0)
        m = m[None, :, None, None, :]
        pr = jax.nn.softmax(jnp.where(m, s, NEG_INF), axis=-1)
        return jnp.einsum('bqgrk,bkgd->bqgrd', pr, vv)

    nqw = S // Q_BLOCK
    qw = qf.reshape(Bsz, nqw, Q_BLOCK, G, R, d).swapaxes(0, 1)
    o_win = lax.map(win_block, (qw, jnp.arange(nqw))).swapaxes(0, 1).reshape(Bsz, S, G, R, d)

    g = jax.nn.sigmoid(g_branch.astype(f32)).reshape(Bsz, S, 3, G, R)[..., None]
    o = g[:, :, 0] * o_cmp + g[:, :, 1] * o_sel + g[:, :, 2] * o_win
    return o.reshape(Bsz, S, ATTN_WIDTH).astype(q.dtype)


def setup_inputs(seed: int = 0) -> dict:
    key = jax.random.key(seed)
    keys = iter(jax.random.split(key, 48))
    nrm = lambda shape, s: jax.random.normal(next(keys), shape, jnp.float32) * s
    gain = lambda shape: 1.0 + nrm(shape, 0.05)
    L = DEPTH
    x = nrm((BATCH, SEQ, D_MODEL), 1.0)
    p = nrm((DEPTH, BATCH, SEQ, PLE_DIM), 1.0)
    dt0 = jnp.exp(jax.random.uniform(next(keys), (L, SSM_HEADS), jnp.float32,
                                     minval=math.log(1e-3), maxval=math.log(1e-1)))
    ssm_dt_bias = dt0 + jnp.log(-jnp.expm1(-dt0))
    ssm_a_log = jnp.log(jax.random.uniform(next(keys), (L, SSM_HEADS), jnp.float32, minval=1.0, maxval=16.0))
    return {
        "x": x,
        "p": p,
        "norm_mix_w": gain((L, D_MODEL)),
        "w_in": nrm((L, D_MODEL, N_IN), D_MODEL ** -0.5),
        "ssm_conv_w": nrm((L, SSM_CONV, SSM_CONV_DIM), SSM_CONV ** -0.5),
        "ssm_conv_b": nrm((L, SSM_CONV_DIM), 0.01),
        "ssm_dt_bias": ssm_dt_bias,
        "ssm_a_log": ssm_a_log,
        "ssm_d": gain((L, SSM_HEADS)),
        "ssm_norm_w": gain((L, D_INNER)),
        "cmp_pe_k": nrm((L, CMP_BLOCK, ATTN_HEAD_DIM), 0.02),
        "cmp_pe_v": nrm((L, CMP_BLOCK, ATTN_HEAD_DIM), 0.02),
        "cmp_wk1": nrm((L, CMP_BLOCK * ATTN_HEAD_DIM, ATTN_HEAD_DIM), (CMP_BLOCK * ATTN_HEAD_DIM) ** -0.5),
        "cmp_wk2": nrm((L, ATTN_HEAD_DIM, ATTN_HEAD_DIM), ATTN_HEAD_DIM ** -0.5),
        "cmp_wv1": nrm((L, CMP_BLOCK * ATTN_HEAD_DIM, ATTN_HEAD_DIM), (CMP_BLOCK * ATTN_HEAD_DIM) ** -0.5),
        "cmp_wv2": nrm((L, ATTN_HEAD_DIM, ATTN_HEAD_DIM), ATTN_HEAD_DIM ** -0.5),
        "w_ssm_branch": nrm((L, D_INNER, D_MODEL), D_INNER ** -0.5),
        "w_attn_branch": nrm((L, ATTN_WIDTH, D_MODEL), ATTN_WIDTH ** -0.5),
        "w_mix_out": nrm((L, D_MODEL, D_MODEL), D_MODEL ** -0.5),
        "norm_ffn_w": gain((L, D_MODEL)),
        "ffn_w_gate": nrm((L, D_MODEL, D_FF), D_MODEL ** -0.5),
        "ffn_w_up": nrm((L, D_MODEL, D_FF), D_MODEL ** -0.5),
        "ffn_conv_w": nrm((L, FFN_CONV, D_FF), FFN_CONV ** -0.5),
        "ffn_conv_b": nrm((L, D_FF), 0.01),
        "ffn_w_down": nrm((L, D_FF, D_MODEL), D_FF ** -0.5),
        "ple_norm_w": gain((L, D_MODEL)),
        "ple_w_gate": nrm((L, D_MODEL, D_MODEL), D_MODEL ** -0.5),
        "ple_w_proj": nrm((L, PLE_DIM, D_MODEL), PLE_DIM ** -0.5),
        "final_norm_w": gain((D_MODEL,)),
    }


def reference(x, p, norm_mix_w, w_in, ssm_conv_w, ssm_conv_b, ssm_dt_bias, ssm_a_log, ssm_d,
              ssm_norm_w, cmp_pe_k, cmp_pe_v, cmp_wk1, cmp_wk2, cmp_wv1, cmp_wv2,
              w_ssm_branch, w_attn_branch, w_mix_out, norm_ffn_w, ffn_w_gate, ffn_w_up,
              ffn_conv_w, ffn_conv_b, ffn_w_down, ple_norm_w, ple_w_gate, ple_w_proj,
              final_norm_w):
    for i in range(DEPTH):
        h = rmsnorm(x, norm_mix_w[i])
        w = w_in[i]
        parts = []
        col = 0
        for sz in IN_SIZES:
            parts.append(h @ w[:, col:col + sz])
            col += sz
        (z, xbc, dt_raw, q, k_c, v_c, k_s, v_s, k_w, v_w,
         g_nsa, g_ssm_merge, g_attn_merge) = parts
        y_ssm = mamba2_mixer(z, xbc, dt_raw, ssm_conv_w[i], ssm_conv_b[i], ssm_dt_bias[i],
                             ssm_a_log[i], ssm_d[i], ssm_norm_w[i])
        y_attn = nsa_mixer(q, k_c, v_c, k_s, v_s, k_w, v_w, g_nsa, cmp_pe_k[i], cmp_pe_v[i],
                           cmp_wk1[i], cmp_wk2[i], cmp_wv1[i], cmp_wv2[i])
        merged = (jax.nn.sigmoid(g_ssm_merge) * (y_ssm @ w_ssm_branch[i])
                  + jax.nn.sigmoid(g_attn_merge) * (y_attn @ w_attn_branch[i]))
        x = x + merged @ w_mix_out[i]
        h = rmsnorm(x, norm_ffn_w[i])
        a = causal_dwconv(h @ ffn_w_gate[i], ffn_conv_w[i], ffn_conv_b[i])
        x = x + (jax.nn.silu(a) * (h @ ffn_w_up[i])) @ ffn_w_down[i]
        ple_gate = jax.nn.sigmoid(rmsnorm(x, ple_norm_w[i]) @ ple_w_gate[i])
        x = x + ple_gate * (p[i] @ ple_w_proj[i])
    return rmsnorm(x, final_norm_w)
```

```python
from concourse.bass_utils import run_bass_kernel_spmd

from collections import defaultdict
from contextlib import ExitStack

import numpy as np
import concourse.bass as bass
import concourse.mybir as mybir

F32 = mybir.dt.float32
BF16 = mybir.dt.bfloat16
ALU = mybir.AluOpType
AF = mybir.ActivationFunctionType
AX = mybir.AxisListType

RING = 8


class Tile:
    __slots__ = ("name", "w", "r")

    def __init__(self, name):
        self.name = name
        self.w = None
        self.r = []


class Op:
    __slots__ = ("eng", "fn", "reads", "writes", "dma", "sigeng", "seq", "waits",
                 "signals", "clock", "sigval", "tag")

    def __init__(self, eng, fn, reads, writes, dma, tag=""):
        self.eng = eng
        self.fn = fn
        self.reads = reads
        self.writes = writes
        self.dma = dma
        self.waits = []
        self.signals = False
        self.sigval = 0
        self.tag = tag


class Prog:
    ENGS = ("pe", "act", "dve", "pool", "sp")

    def __init__(self, nc):
        self.nc = nc
        self.ops = []

    def op(self, eng, fn, reads=(), writes=(), dma=False, tag=""):
        o = Op(eng, fn, tuple(reads), tuple(writes), dma, tag)
        self.ops.append(o)
        return o

    def barrier(self):
        self.ops.append("BARRIER")

    def analyze(self):
        seq = defaultdict(int)
        known = defaultdict(dict)
        ring_cnt = defaultdict(int)
        ring_last = {}
        last_of = {}
        real_ops = []
        for op in self.ops:
            if isinstance(op, str):
                frontier = list(last_of.values())
                for E in self.ENGS:
                    b = Op(E, None, (), (), False, "barrier")
                    b.sigeng = E
                    seq[E] += 1
                    b.seq = seq[E]
                    self._resolve(b, frontier, known)
                    real_ops.append(b)
                continue
            E = op.eng
            if op.dma:
                k = ring_cnt[E] % RING
                ring_cnt[E] += 1
                op.sigeng = "%s.d%d" % (E, k)
            else:
                op.sigeng = E
            seq[op.sigeng] += 1
            op.seq = seq[op.sigeng]
            deps = []
            for t in op.reads:
                if t.w is not None:
                    deps.append(t.w)
            for t in op.writes:
                if t.w is not None:
                    deps.append(t.w)
                deps.extend(t.r)
            if op.dma:
                if op.sigeng in ring_last:
                    deps.append(ring_last[op.sigeng])
                ring_last[op.sigeng] = op
            self._resolve(op, deps, known)
            if op.dma:
                op.signals = True
            for t in op.reads:
                t.r.append(op)
            for t in op.writes:
                t.w = op
                t.r = []
            last_of[op.sigeng] = op
            real_ops.append(op)
        self.real_ops = real_ops
        cnt = defaultdict(int)
        for op in real_ops:
            if op.signals:
                cnt[op.sigeng] += 1
                op.sigval = cnt[op.sigeng] * (16 if op.dma else 1)
        self.sigengs = sorted(set(o.sigeng for o in real_ops if o.signals))

    @staticmethod
    def _resolve(op, deps, known):
        E = op.eng
        kn = known[E]
        need = {}
        for d in deps:
            if d is op:
                continue
            F = d.sigeng
            if F == "pe" and E == "pe" and not op.dma and not d.dma:
                continue
            if kn.get(F, 0) >= d.seq:
                continue
            if F not in need or need[F].seq < d.seq:
                need[F] = d
        for F, d in need.items():
            op.waits.append(d)
            d.signals = True
            for G, v in d.clock.items():
                if kn.get(G, 0) < v:
                    kn[G] = v
        op.clock = dict(kn)
        op.clock[op.sigeng] = op.seq

    def emit(self):
        nc = self.nc
        self.analyze()
        with ExitStack() as es:
            sems = {}
            for se in self.sigengs:
                sems[se] = es.enter_context(nc.semaphore("s_" + se.replace(".", "_")))
            block = es.enter_context(nc.Block())
            by_eng = defaultdict(list)
            for op in self.real_ops:
                by_eng[op.eng].append(op)

            def runner(ops):
                def run(e):
                    for op in ops:
                        for d in op.waits:
                            e.wait_ge(sems[d.sigeng], d.sigval)
                        if op.fn is None:
                            continue
                        ins = op.fn(e)
                        if op.signals:
                            ins.then_inc(sems[op.sigeng], 16 if op.dma else 1)
                return run

            if by_eng["pe"]:
                block.tensor(runner(by_eng["pe"]))
            if by_eng["act"]:
                block.scalar(runner(by_eng["act"]))
            if by_eng["dve"]:
                block.vector(runner(by_eng["dve"]))
            if by_eng["pool"]:
                block.gpsimd(runner(by_eng["pool"]))
            if by_eng["sp"]:
                block.sync(runner(by_eng["sp"]))

    def stats(self):
        c = defaultdict(int)
        w = defaultdict(int)
        for op in self.real_ops:
            c[op.eng] += 1
            w[op.eng] += len(op.waits)
        return dict(c), dict(w)


D = 2048; S = 2048; NIN = 19568
C_Z = 0; C_XBC = 4096; C_DT = 10240; C_Q = 10304; C_KC = 12352; C_VC = 12864
C_KS = 13376; C_VS = 13888; C_KW = 14400; C_VW = 14912; C_GN = 15424; C_GS = 15472; C_GA = 17520
DFF = 5632
EPS = 1e-6


def bc_mid(a, n):
    return bass.AP(a.tensor, a.offset, [list(a.ap[0]), [0, n]] + [list(x) for x in a.ap[1:]])


def bc_last(a, n):
    return bass.AP(a.tensor, a.offset, [list(x) for x in a.ap] + [[0, n]])


class B(Prog):
    def mm(self, out, lhsT, rhs, start, stop, reads, writes):
        return self.op("pe", lambda e: e.matmul(out, lhsT, rhs, start=start, stop=stop), reads, writes)

    def tr(self, out, in_, ident, reads, writes):
        return self.op("pe", lambda e: e.transpose(out, in_, ident), reads, writes)

    def act(self, out, in_, func, reads, writes, bias=0.0, scale=1.0, accum_out=None):
        if accum_out is None:
            return self.op("act", lambda e: e.activation(out, in_, func, bias=bias, scale=scale), reads, writes)
        return self.op("act", lambda e: e.activation(out, in_, func, bias=bias, scale=scale, accum_out=accum_out), reads, writes)

    def tt(self, eng, out, in0, in1, op, reads, writes):
        return self.op(eng, lambda e: e.tensor_tensor(out, in0, in1, op), reads, writes)

    def ts(self, eng, out, in0, s1, s2, op0, op1, reads, writes):
        if s2 is None:
            return self.op(eng, lambda e: e.tensor_scalar(out, in0, s1, None, op0), reads, writes)
        return self.op(eng, lambda e: e.tensor_scalar(out, in0, s1, s2, op0, op1), reads, writes)

    def stt(self, eng, out, in0, scalar, in1, op0, op1, reads, writes):
        return self.op(eng, lambda e: e.scalar_tensor_tensor(out, in0, scalar, in1, op0, op1), reads, writes)

    def copy(self, eng, out, in_, reads, writes):
        if eng == "act":
            return self.op(eng, lambda e: e.copy(out, in_), reads, writes)
        return self.op(eng, lambda e: e.tensor_copy(out, in_), reads, writes)

    def recip(self, out, in_, reads, writes):
        return self.op("dve", lambda e: e.reciprocal(out, in_), reads, writes)

    def dma(self, eng, out, in_, reads, writes):
        return self.op(eng, lambda e: e.dma_start(out=out, in_=in_), reads, writes, dma=True)

    def memset(self, eng, ap, val, writes):
        return self.op(eng, lambda e: e.memset(ap, val), (), writes)


def T(name):
    return Tile(name)


def sb(es, nc, name, shape, dt):
    return es.enter_context(nc.sbuf_tensor(name, list(shape), dt))


def norm_to_T(P, nc, es, ps, ident, src_rows, wrow, wrow_t, dstT, dstT_tiles, pfx, ident_t):
    xb = [sb(es, nc, pfx + "xb%d" % i, [128, 2048], F32) for i in range(2)]
    xb_t = [T(pfx + "xb%d" % i) for i in range(2)]
    junk = sb(es, nc, pfx + "junk", [128, 2048], BF16); junk_t = T("junk")
    xn = [sb(es, nc, pfx + "xn%d" % i, [128, 2048], BF16) for i in range(2)]
    xn_t = [T(pfx + "xn%d" % i) for i in range(2)]
    st = sb(es, nc, pfx + "st", [128, 16, 4], F32)
    st_t = [T(pfx + "st%d" % i) for i in range(16)]
    psb = [ps[0][0].bitcast(BF16), ps[1][0].bitcast(BF16)]
    for tt in range(16):
        i = tt % 2
        ap, rt = src_rows(tt)
        P.dma("sp", xb[i][:, :], ap, rt, [xb_t[i]])
        P.act(junk[:, :], xb[i][:, :], AF.Square, [xb_t[i]], [junk_t, st_t[tt]], accum_out=st[:, tt, 0:1])
        P.act(st[:, tt, 1:2], st[:, tt, 0:1], AF.Sqrt, [st_t[tt]], [st_t[tt]], bias=EPS, scale=1.0 / D)
        P.recip(st[:, tt, 2:3], st[:, tt, 1:2], [st_t[tt]], [st_t[tt]])
        P.stt("dve", xn[i][:, :], xb[i][:, :], st[:, tt, 2:3], wrow[:, :], ALU.mult, ALU.mult,
              [xb_t[i], st_t[tt], wrow_t], [xn_t[i]])
        for half in range(2):
            pt = psb[half]
            for j in range(8):
                kc = half * 8 + j
                P.tr(pt[:, j * 128:(j + 1) * 128], xn[i][:, kc * 128:(kc + 1) * 128], ident[:, :],
                     [xn_t[i], ident_t], [ps[half][1]])
            src = pt[:, :].rearrange("p (a b) -> p a b", a=8)
            eng = "act" if half == 0 else "dve"
            P.copy(eng, dstT[:, half * 8:half * 8 + 8, tt * 128:(tt + 1) * 128], src,
                   [ps[half][1]], [dstT_tiles[tt]])


def load_consts(P, nc, es, cin):
    C = {}
    def ld(name, shape, dt, src, eng):
        t = sb(es, nc, "k_" + name, shape, dt); tl = T("k_" + name)
        P.dma(eng, t.ap() if False else t[tuple(slice(None) for _ in shape)], src, [], [tl])
        C[name] = (t, tl)
    ld("ident", [128, 128], BF16, cin["c_ident"], "pool")
    ld("U", [128, 128], F32, cin["c_U"], "sp")
    ld("onesf", [128, 128], F32, cin["c_onesf"], "sp")
    ld("maskneg", [128, 128], F32, cin["c_maskneg"], "sp")
    ld("identf", [128, 128], F32, cin["c_ident"], "sp")
    return C


def ssm_phase(P, nc, es, ps, C, hT, hT_t, dr, y_ssmT, ysT_tiles, ac_scr):
    pa, pa_t, q, q_t = ps
    w_in = dr["w_in"]
    ident, ident_t = C["ident"]; U, U_t = C["U"]; onesf, onesf_t = C["onesf"]; mneg, mneg_t = C["maskneg"]
    cw = sb(es, nc, "cw", [128, 48, 4], F32); cw_t = T("cw")
    cb = sb(es, nc, "cb", [128, 48, 1], F32); cb_t = T("cb")
    P.dma("sp", cw[:, :, :], dr["ssm_cw"].rearrange("p (j k) -> p j k", k=4), [], [cw_t])
    P.dma("sp", cb[:, :, :], dr["ssm_cb"].rearrange("p (j k) -> p j k", k=1), [], [cb_t])
    rows = sb(es, nc, "rows", [128, 4, 64], F32); rows_t = T("rows")
    P.dma("sp", rows[:, 0, :], dr["ssm_dt_bias"].broadcast_to([128, 64]), [], [rows_t])
    P.dma("sp", rows[:, 1, :], dr["ssm_a_log"].broadcast_to([128, 64]), [rows_t], [rows_t])
    P.dma("sp", rows[:, 2, :], dr["ssm_d"].broadcast_to([128, 64]), [rows_t], [rows_t])
    P.act(rows[:, 3, :], rows[:, 1, :], AF.Exp, [rows_t], [rows_t])
    P.ts("dve", rows[:, 3, :], rows[:, 3, :], -1.0, None, ALU.mult, None, [rows_t], [rows_t])
    dtb = rows[:, 0, :]; nega = rows[:, 3, :]; dsk = rows[:, 2, :]
    epsb = sb(es, nc, "epsb", [128, 1], F32)
    P.memset("dve", epsb[:, :], EPS, [rows_t])

    dt_all = sb(es, nc, "dt_all", [128, 16, 64], F32)
    eac_all = sb(es, nc, "eac_all", [128, 16, 64], F32)
    nacum_all = sb(es, nc, "nacum_all", [128, 16, 64], F32)
    es_p = ExitStack()
    acum_all = sb(es_p, nc, "acum_all", [128, 16, 64], F32)
    adt_all = sb(es_p, nc, "adt_all", [128, 16, 64], F32)
    acT = sb(es_p, nc, "acT", [64, 2048], F32); acT_t = T("acT")
    identf, identf_t = C["identf"]
    dts_t = [T("dts%d" % i) for i in range(16)]
    wdt = sb(es_p, nc, "wdt", [128, 16, 64], BF16); wdt_t = T("wdt")
    P.dma("pool", wdt[:, :, :], w_in[:, C_DT:C_DT + 64].rearrange("(kc p) n -> p kc n", p=128), [], [wdt_t])
    tmp64 = sb(es_p, nc, "tmp64", [128, 2, 64], F32); tmp64_t = T("tmp64")
    for tt in range(16):
        pt, ptt = q[tt % 2], q_t[tt % 2]
        for kc in range(16):
            P.mm(pt[:, 0:64], hT[:, kc, tt * 128:(tt + 1) * 128], wdt[:, kc, :], kc == 0, kc == 15,
                 [hT_t[tt], wdt_t], [ptt])
        P.tt("dve", tmp64[:, 0, :], pt[:, 0:64], dtb, ALU.add, [ptt, rows_t], [tmp64_t])
        P.act(tmp64[:, 1, :], tmp64[:, 0, :], AF.Exp, [tmp64_t], [tmp64_t])
        P.act(dt_all[:, tt, :], tmp64[:, 1, :], AF.Ln, [tmp64_t], [dts_t[tt]], bias=1.0)
        P.tt("dve", adt_all[:, tt, :], dt_all[:, tt, :], nega, ALU.mult, [dts_t[tt], rows_t], [dts_t[tt]])
        P.mm(q[2][:, 0:64], U[:, :], adt_all[:, tt, :], True, True, [U_t, dts_t[tt]], [q_t[2]])
        P.copy("dve", acum_all[:, tt, :], q[2][:, 0:64], [q_t[2]], [dts_t[tt]])
        P.act(eac_all[:, tt, :], acum_all[:, tt, :], AF.Exp, [dts_t[tt]], [dts_t[tt]])
        P.ts("dve", nacum_all[:, tt, :], acum_all[:, tt, :], -1.0, None, ALU.mult, None, [dts_t[tt]], [dts_t[tt]])
        P.tr(q[3][0:64, 0:128], acum_all[:, tt, :], identf[:, :], [dts_t[tt], identf_t], [q_t[3]])
        P.copy("act", acT[:, tt * 128:(tt + 1) * 128], q[3][0:64, 0:128], [q_t[3]], [acT_t])
    ac_t = T("ac_scr")
    P.dma("sp", ac_scr.rearrange("c g r l -> (g r) c l"), acT[:, :].rearrange("p (c l) -> p c l", c=16), [acT_t], [ac_t])
    es_p.close()
    P.barrier()

    wz = sb(es, nc, "wz", [128, 16, 512], BF16); wz_t = T("wz")
    nwg = sb(es, nc, "nwg", [128, 512], F32); nwg_t = T("nwg")
    wch = [sb(es, nc, "wch%d" % i, [128, 16, 128], BF16) for i in range(2)]
    wch_t = [T("wch%d" % i) for i in range(2)]
    xsT2 = [(sb(es, nc, "xsT_%d" % p_, [128, 4, 2048], BF16), [T("xsT%d_%d" % (p_, i)) for i in range(4)]) for p_ in range(2)]
    bmT2 = [(sb(es, nc, "bmT_%d" % p_, [128, 2048], BF16), [T("bmT%d_%d" % (p_, i)) for i in range(4)]) for p_ in range(2)]
    cmT2 = [(sb(es, nc, "cmT_%d" % p_, [128, 2048], BF16), [T("cmT%d_%d" % (p_, i)) for i in range(4)]) for p_ in range(2)]
    raw = [sb(es, nc, "raw%d" % i, [128, 515], F32) for i in range(2)]; raw_t = [T("raw%d" % i) for i in range(2)]
    acc = [sb(es, nc, "acc%d" % i, [128, 512], F32) for i in range(2)]; acc_t = [T("acc%d" % i) for i in range(2)]
    arows = [sb(es, nc, "arow%d" % i, [128, 8, 128], F32) for i in range(2)]; arows_t = [T("arow%d" % i) for i in range(2)]
    segb = sb(es, nc, "segb", [128, 8, 128], F32); segb_t = T("segb")
    decs = [sb(es, nc, "dec%d" % i, [128, 8, 128], BF16) for i in range(2)]; decs_t = [T("dec%d" % i) for i in range(2)]
    szs = [sb(es, nc, "szs%d" % i, [128, 512], BF16) for i in range(3)]; szs_t = [T("szs%d" % i) for i in range(3)]
    zt = sb(es, nc, "zt", [128, 512], F32); zt_t = T("zt")
    t3s = [sb(es, nc, "t3s%d" % i, [128, 8, 64], BF16) for i in range(3)]; t3s_t = [T("t3s%d" % i) for i in range(3)]
    cbs = [sb(es, nc, "cbs%d" % i, [128, 128], BF16) for i in range(2)]; cbs_t = [T("cbs%d" % i) for i in range(2)]
    xss = [sb(es, nc, "xss%d" % i, [128, 8, 64], BF16) for i in range(3)]; xss_t = [T("xss%d" % i) for i in range(3)]
    bms = [sb(es, nc, "bms%d" % i, [128, 128], BF16) for i in range(2)]; bms_t = [T("bms%d" % i) for i in range(2)]
    sms = [sb(es, nc, "sms%d" % i, [128, 4, 8], F32) for i in range(2)]; sms_t = [T("sms%d" % i) for i in range(2)]
    t1s = [sb(es, nc, "t1s%d" % i, [128, 8, 64], F32) for i in range(2)]; t1s_t = [T("t1s%d" % i) for i in range(2)]
    MT = sb(es, nc, "MT", [128, 8, 128], BF16); MT_t = T("MT")
    x_bf = sb(es, nc, "x_bf", [128, 8, 64], BF16); x_bf_t = T("x_bf")
    xd = sb(es, nc, "xd", [128, 8, 64], BF16); xd_t = T("xd")
    yg = sb(es, nc, "yg", [128, 512], F32); yg_t = T("yg")
    st = sb(es, nc, "sst", [128, 4], F32); st_t = T("sst")
    yn = sb(es, nc, "yn", [128, 512], BF16); yn_t = T("yn")
    yTs = [sb(es, nc, "yTs%d" % i, [128, 4, 128], BF16) for i in range(2)]; yTs_t = [T("yTs%d" % i) for i in range(2)]
    hst = sb(es, nc, "hst", [128, 8, 64], F32); hst_t = T("hst")
    hpb = sb(es, nc, "hpb", [128, 512], BF16); hpb_t = T("hpb")
    q3b = q[3].bitcast(BF16)
    q2b = q[2].bitcast(BF16)
    q4b = q[4].bitcast(BF16)
    q3x_t = T("q3x"); q3y_t = T("q3y"); q2c_t = q_t[2]; q2b_t = T("q2b")
    pa0_t, pa1_t = pa_t
    wi = [0]

    tails = [sb(es, nc, "tail%d" % i, [128, 3], F32) for i in range(3)]; tails_t = [T("tail%d" % i) for i in range(3)]

    def make_proj(gg):
        xsT_, xsTt_ = xsT2[gg % 2]; bmT_, bmTt_ = bmT2[gg % 2]; cmT_, cmTt_ = cmT2[gg % 2]
        chunks = [(C_XBC + gg * 512 + j * 128, gg * 4 + j, ("xs", j)) for j in range(4)]
        chunks.append((C_XBC + 4096 + gg * 128, 32 + gg, ("bm", 0)))
        chunks.append((C_XBC + 5120 + gg * 128, 40 + gg, ("cm", 0)))
        tiles = []
        for (col, ch, (kind, j)) in chunks:
            hold = {}
            for t4 in range(4):
                def mk(col=col, ch=ch, kind=kind, j=j, t4=t4, hold=hold):
                    pt = pa[:, (t4 % 2) * 512:(t4 % 2 + 1) * 512]; ptt = pa_t[t4 % 2]
                    rw, rwt = raw[t4 % 2], raw_t[t4 % 2]
                    ac, act_ = acc[t4 % 2], acc_t[t4 % 2]
                    sc_ = rw[:, 3:515]
                    if kind == "xs":
                        dst, dt_ = xsT_[:, j, t4 * 512:(t4 + 1) * 512], xsTt_[t4]
                    elif kind == "bm":
                        dst, dt_ = bmT_[:, t4 * 512:(t4 + 1) * 512], bmTt_[t4]
                    else:
                        dst, dt_ = cmT_[:, t4 * 512:(t4 + 1) * 512], cmTt_[t4]

                    def p1():
                        if t4 == 0:
                            wb, wbt = wch[wi[0] % 2], wch_t[wi[0] % 2]; wi[0] += 1
                            hold["w"] = (wb, wbt)
                            P.dma("pool", wb[:, :, :], w_in[:, col:col + 128].rearrange("(kc p) n -> p kc n", p=128), [], [wbt])
                        wb, wbt = hold["w"]
                        for kc in range(16):
                            P.mm(pt, wb[:, kc, :], hT[:, kc, t4 * 512:(t4 + 1) * 512], kc == 0, kc == 15,
                                 [wbt] + hT_t[t4 * 4:t4 * 4 + 4], [ptt])

                    def p2():
                        P.copy("act", rw[:, 3:515], pt, [ptt], [rwt])
                        if t4 < 3:
                            P.copy("act", tails[t4][:, :], rw[:, 512:515], [rwt], [tails_t[t4]])

                    def p3():
                        if t4 == 0:
                            P.memset("dve", rw[:, 0:3], 0.0, [rwt])
                        else:
                            P.copy("dve", rw[:, 0:3], tails[t4 - 1][:, :], [tails_t[t4 - 1]], [rwt])
                        P.ts("dve", ac[:, :], rw[:, 3:515], cw[:, ch, 3:4], cb[:, ch, 0:1], ALU.mult, ALU.add,
                             [rwt, cw_t, cb_t], [act_])
                        for k in (2, 1, 0):
                            P.stt("dve", ac[:, :], rw[:, k:k + 512], cw[:, ch, k:k + 1], ac[:, :], ALU.mult, ALU.add,
                                  [rwt, cw_t, act_], [act_])

                    def p4():
                        P.act(sc_, ac[:, :], AF.Exp, [act_, rwt], [rwt], scale=-1.0)
                        P.act(sc_, sc_, AF.Ln, [rwt], [rwt], bias=1.0)
                        P.act(sc_, sc_, AF.Exp, [rwt], [rwt], scale=-1.0)

                    def p5():
                        P.tt("dve", dst, ac[:, :], sc_, ALU.mult, [act_, rwt], [dt_])
                    return (p1, p2, p3, p4, p5)
                tiles.append(mk())
        return tiles

    def run_tiles(tl):
        for t_ in tl:
            for f_ in t_:
                f_()

    run_tiles(make_proj(0))
    for g in range(8):
        xsT, xsT_t = xsT2[g % 2]; bmT, bmT_t = bmT2[g % 2]; cmT, cmT_t = cmT2[g % 2]
        P.dma("pool", wz[:, :, :], w_in[:, C_Z + g * 512:C_Z + (g + 1) * 512].rearrange("(kc p) n -> p kc n", p=128), [], [wz_t])
        P.dma("sp", nwg[:, :], dr["ssm_norm_w"][:, g * 512:(g + 1) * 512].broadcast_to([128, 512]), [], [nwg_t])
        ptiles = make_proj(g + 1) if g + 1 < 8 else []
        pidx = 0
        P.memset("dve", hst[:, :, :], 0.0, [hst_t])
        P.memset("dve", hpb[:, :], 0.0, [hpb_t])
        hs = slice(g * 8, (g + 1) * 8)

        def S1(c):
            t4 = c // 4
            tk = slice(c * 128, (c + 1) * 128)
            ar_, art_ = arows[c % 2], arows_t[c % 2]
            sm_, smt_ = sms[c % 2], sms_t[c % 2]
            for cc in ([0, 1] if c == 0 else ([c + 1] if c + 1 < 16 else [])):
                asrc = bass.AP(ac_scr.tensor, ac_scr.offset + ((cc * 8 + g) * 8) * 128, [[0, 128], [1, 1024]])
                P.dma("sp", arows[cc % 2][:, :, :].rearrange("p r l -> p (r l)"), asrc, [ac_t], [arows_t[cc % 2]])
            for kc in range(16):
                P.mm(q[0][:, :], hT[:, kc, tk], wz[:, kc, :], kc == 0, kc == 15, [hT_t[c], wz_t], [q_t[0]])
            P.act(zt[:, :], q[0][:, :], AF.Exp, [q_t[0]], [zt_t], scale=-1.0)
            P.act(zt[:, :], zt[:, :], AF.Ln, [zt_t], [zt_t], bias=1.0)
            P.act(zt[:, :], zt[:, :], AF.Exp, [zt_t], [zt_t], scale=-1.0)
            P.tt("dve", szs[c % 3][:, :], q[0][:, :], zt[:, :], ALU.mult, [q_t[0], zt_t], [szs_t[c % 3]])
            P.mm(q[2][:, 0:128], bmT[:, tk], cmT[:, tk], True, True, [bmT_t[t4], cmT_t[t4]], [q2c_t])
            P.tt("dve", cbs[c % 2][:, :], q[2][:, 0:128], U[:, :], ALU.mult, [q2c_t, U_t], [cbs_t[c % 2]])
            P.tt("pool", segb[:, :, :], ar_[:, :, :], bc_last(nacum_all[:, c, hs], 128), ALU.add, [art_, dts_t[c]], [segb_t])
            P.act(sm_[:, 2, :], ar_[:, :, 127], AF.Exp, [art_], [smt_])
            P.act(segb[:, :, :], segb[:, :, :], AF.Relu, [segb_t], [segb_t], scale=-1.0)
            P.act(decs[c % 2][:, :, :], segb[:, :, :], AF.Exp, [segb_t], [decs_t[c % 2]], scale=-1.0)
            for j in range(4):
                P.tr(q3b[:, j * 128:(j + 1) * 128], xsT[:, j, tk], ident[:, :], [xsT_t[t4], ident_t], [q3x_t])
            P.tr(q3b[:, 512:640], bmT[:, tk], ident[:, :], [bmT_t[t4], ident_t], [q3x_t])
            P.copy("act", xss[c % 3][:, :, :], q3b[:, 0:512].rearrange("p (r d) -> p r d", r=8), [q3x_t], [xss_t[c % 3]])
            P.copy("act", bms[c % 2][:, :], q3b[:, 512:640], [q3x_t], [bms_t[c % 2]])
            P.tt("pool", t3s[c % 3][:, :, :], xss[c % 3][:, :, :], bc_last(dsk[:, hs], 64), ALU.mult, [xss_t[c % 3], rows_t], [t3s_t[c % 3]])

        def S2a(c):
            t4 = c // 4
            tk = slice(c * 128, (c + 1) * 128)
            dtg = dt_all[:, c, hs]
            sm_, smt_ = sms[c % 2], sms_t[c % 2]
            xs_, xst_ = xss[c % 3], xss_t[c % 3]
            P.tt("dve", MT[:, :, :], decs[c % 2][:, :, :], bc_mid(cbs[c % 2][:, :], 8), ALU.mult,
                 [decs_t[c % 2], cbs_t[c % 2]], [MT_t])
            P.tt("dve", x_bf[:, :, :], xs_[:, :, :], bc_last(dtg, 64), ALU.mult, [xst_, dts_t[c]], [x_bf_t])
            P.tt("dve", xd[:, :, :], x_bf[:, :, :], bc_last(decs[c % 2][:, :, 127], 64), ALU.mult, [x_bf_t, decs_t[c % 2]], [xd_t])
            for r in range(8):
                P.mm(q[4][:, r * 64:(r + 1) * 64], MT[:, r, :], x_bf[:, r, :], True, True, [MT_t, x_bf_t], [q_t[4]])
            P.mm(q[5][:, :], cmT[:, tk], hpb[:, :], True, True, [cmT_t[t4], hpb_t], [q_t[5]])
            P.mm(q[1][:, :], bms[c % 2][:, :], xd[:, :, :], True, True, [bms_t[c % 2], xd_t], [q_t[1]])

        def S2b(c):
            eag = eac_all[:, c, hs]
            sm_, smt_ = sms[c % 2], sms_t[c % 2]
            t1_, t1t_ = t1s[c % 2], t1s_t[c % 2]
            P.tt("dve", t1_[:, :, :], q[5][:, :].rearrange("p (r d) -> p r d", r=8), bc_last(eag, 64), ALU.mult,
                 [q_t[5], dts_t[c]], [t1t_])
            P.tt("dve", t1_[:, :, :], t1_[:, :, :], q[4][:, :].rearrange("p (r d) -> p r d", r=8), ALU.add, [t1t_, q_t[4]], [t1t_])
            P.tt("dve", hst[:, :, :], hst[:, :, :], bc_last(sm_[:, 2, :], 64), ALU.mult, [hst_t, smt_], [hst_t])
            P.tt("dve", hst[:, :, :], hst[:, :, :], q[1][:, :].rearrange("p (r d) -> p r d", r=8), ALU.add, [hst_t, q_t[1]], [hst_t])
            P.copy("act", hpb[:, :], hst[:, :, :].rearrange("p r d -> p (r d)"), [hst_t], [hpb_t])

        def S3(c):
            tk = slice(c * 128, (c + 1) * 128)
            xs_, xst_ = xss[c % 3], xss_t[c % 3]
            t1_, t1t_ = t1s[c % 2], t1s_t[c % 2]
            P.tt("dve", t1_[:, :, :], t1_[:, :, :], t3s[c % 3][:, :, :], ALU.add, [t1t_, t3s_t[c % 3]], [t1t_])
            P.tt("dve", yg[:, :], t1_[:, :, :].rearrange("p r d -> p (r d)"), szs[c % 3][:, :], ALU.mult, [t1t_, szs_t[c % 3]], [yg_t])
            P.act(yn[:, :], yg[:, :], AF.Square, [yg_t], [yn_t, st_t], accum_out=st[:, 0:1])
            P.act(st[:, 1:2], st[:, 0:1], AF.Ln, [st_t], [st_t], bias=epsb[:, 0:1], scale=1.0 / 512)
            P.act(st[:, 2:3], st[:, 1:2], AF.Exp, [st_t], [st_t], scale=-0.5)
            P.stt("dve", yn[:, :], yg[:, :], st[:, 2:3], nwg[:, :], ALU.mult, ALU.mult, [yg_t, st_t, nwg_t], [yn_t])
            for j in range(4):
                P.tr(q3b[:, j * 128:(j + 1) * 128], yn[:, j * 128:(j + 1) * 128], ident[:, :], [yn_t, ident_t], [q3x_t])
            ys, yst = yTs[c % 2], yTs_t[c % 2]
            P.copy("act", ys[:, :, :], q3b[:, 0:512].rearrange("p (j t) -> p j t", j=4), [q3x_t], [yst])
            P.dma("sp", y_ssmT[g * 512:(g + 1) * 512, tk].rearrange("(j p) t -> p j t", p=128), ys[:, :, :], [yst], [ysT_tiles[g][c]])

        for i in range(16 + 2):
            cur = ptiles[pidx:pidx + 2]; pidx += 2
            for t_ in cur:
                t_[0]()
            if i < 16:
                S1(i)
            for t_ in cur:
                t_[1]()
            if 0 <= i - 1 < 16:
                S2a(i - 1)
            for t_ in cur:
                t_[2]()
            if 0 <= i - 2 < 16:
                S3(i - 2)
            for t_ in cur:
                t_[3]()
            if 0 <= i - 1 < 16:
                S2b(i - 1)
            for t_ in cur:
                t_[4]()
        run_tiles(ptiles[pidx:])


def attn_phase(P, nc, es, ps, C, hT, hT_t, dr, y_attnT, yaT_tiles, sg_scr):
    pa, pa_t, q, q_t = ps
    pa0_t, pa1_t = pa_t
    w_in = dr["w_in"]
    ident, ident_t = C["ident"]
    SC = 1.0 / (128.0 ** 0.5)

    def cld(name, shape, dt, src, eng):
        t = sb(es, nc, "a_" + name, shape, dt); tl = T("a_" + name)
        P.dma(eng, t[tuple(slice(None) for _ in shape)], src, [], [tl])
        return t, tl
    identf, identf_t = cld("identf", [128, 128], F32, dr["c_ident"], "sp")
    RmT, RmT_t = cld("RmT", [128, 32], BF16, dr["c_RmT"], "pool")
    cosT, cosT_t = cld("cosT", [32, 2048], F32, dr["c_cos"], "sp")
    sinT, sinT_t = cld("sinT", [32, 2048], F32, dr["c_sin"], "sp")
    cmk, cmk_t = cld("cmk", [128, 2048], BF16, dr["c_cmask"], "pool")
    Asc, Asc_t = cld("Asc", [128, 16, 32], F32, dr["c_A"].rearrange("(t p) j -> p t j", p=128), "sp")
    Bsc, Bsc_t = cld("Bsc", [128, 16, 32], F32, dr["c_B"].rearrange("(t p) j -> p t j", p=128), "sp")
    Esel, Esel_t = cld("Esel", [128, 16, 128], BF16, dr["c_E"], "pool")
    dmask, dmask_t = cld("dmask", [128, 128], BF16, dr["c_diag"], "pool")
    lmask, lmask_t = cld("lmask", [128, 128], BF16, dr["c_low"], "pool")
    onesb, onesb_t = cld("onesb", [128, 128], BF16, dr["c_onesf"], "pool")
    mapa, mapa_t = cld("mapa", [128, 33], BF16, dr["c_map"], "pool")
    w1k, w1k_t = cld("w1k", [128, 32, 128], BF16, dr["cmp_wk1"].rearrange("(l d) m -> d l m", d=128), "pool")
    w1v, w1v_t = cld("w1v", [128, 32, 128], BF16, dr["cmp_wv1"].rearrange("(l d) m -> d l m", d=128), "pool")
    w2k, w2k_t = cld("w2k", [128, 128], BF16, dr["cmp_wk2"], "pool")
    w2v, w2v_t = cld("w2v", [128, 128], BF16, dr["cmp_wv2"], "pool")
    pek, pek_t = cld("pek", [128, 32], BF16, dr["cmp_pekT"], "pool")
    pev, pev_t = cld("pev", [128, 32], BF16, dr["cmp_pevT"], "pool")
    wch = [sb(es, nc, "awch%d" % i, [128, 16, 128], BF16) for i in range(2)]
    wch_t = [T("awch%d" % i) for i in range(2)]
    wi = [0]

    def load_w(col, n):
        wb, wbt = wch[wi[0] % 2], wch_t[wi[0] % 2]; wi[0] += 1
        P.dma("pool", wb[:, :, 0:n], w_in[:, col:col + n].rearrange("(kc p) n -> p kc n", p=128), [], [wbt])
        return wb, wbt

    sgT = sb(es, nc, "sgT", [48, 2048], BF16); sgT_t = T("sgT")
    wb, wbt = load_w(C_GN, 48)
    for t4 in range(4):
        pt, ptt = q[t4 % 2], q_t[t4 % 2]
        for kc in range(16):
            P.mm(pt[0:48, :], wb[:, kc, 0:48], hT[:, kc, t4 * 512:(t4 + 1) * 512], kc == 0, kc == 15,
                 [wbt] + hT_t[t4 * 4:t4 * 4 + 4], [ptt])
        P.act(sgT[:, t4 * 512:(t4 + 1) * 512], pt[0:48, :], AF.Sigmoid, [ptt], [sgT_t])
    sg_t = T("sg_scr")
    P.dma("sp", sg_scr, sgT[:, :], [sgT_t], [sg_t])

    cst = sb(es, nc, "cst", [128, 2], F32); cst_t = T("cst")
    for idx, (w1, w1_t, pe, pe_t) in enumerate(((w1k, w1k_t, pek, pek_t), (w1v, w1v_t, pev, pev_t))):
        for l in range(32):
            P.mm(q[2][:, 0:1], w1[:, l, :], pe[:, l:l + 1], l == 0, l == 31, [w1_t, pe_t], [q_t[2]])
        P.copy("dve", cst[:, idx:idx + 1], q[2][:, 0:1], [q_t[2]], [cst_t])

    qT = sb(es, nc, "qT", [128, 4, 2048], BF16); qT_t = [T("qT%d" % i) for i in range(4)]
    kT = {}
    for nm in ("kc", "ks", "kw", "vc"):
        kT[nm] = (sb(es, nc, nm + "T", [128, 2048], BF16), [T(nm + "T%d" % i) for i in range(4)])
    vtm = {}
    for nm in ("vs", "vw"):
        vtm[nm] = (sb(es, nc, nm + "tm", [128, 16, 128], BF16), [T(nm + "tm%d" % i) for i in range(16)])
    rtmp = [sb(es, nc, "rtmp%d" % i, [32, 512], F32) for i in range(2)]; rtmp_t = [T("rtmp%d" % i) for i in range(2)]
    hid = [sb(es, nc, "hid%d" % i, [128, 128], BF16) for i in range(2)]; hid_t = [T("hid%d" % i) for i in range(2)]
    kcmpT = sb(es, nc, "kcmpT", [128, 128], BF16); kcmpT_t = T("kcmpT")
    vcmp = sb(es, nc, "vcmp", [128, 128], BF16); vcmp_t = T("vcmp")
    Pc = sb(es, nc, "Pc", [128, 4, 128], BF16); Pc_t = T("Pc")
    PTb = [sb(es, nc, "PT%d" % i, [128, 4, 128], BF16) for i in range(3)]; PTb_t = [T("PT%d" % i) for i in range(3)]
    gb = [sb(es, nc, "gb%d" % i, [128, 3, 4, 128], BF16) for i in range(2)]; gb_t = [T("gb%d" % i) for i in range(2)]
    smi = sb(es, nc, "smi", [128, 8], F32); smi_t = T("smi")
    imp = sb(es, nc, "imp", [128, 4, 32], F32); imp_t = T("imp")
    m8 = sb(es, nc, "m8", [128, 2, 8], F32); m8_t = T("m8")
    mbf = sb(es, nc, "mbf", [128, 32], F32); mbf_t = T("mbf")
    mbT = sb(es, nc, "mbT", [128, 4, 128], BF16); mbT_t = T("mbT")
    P.memset("pool", mbT[:, :, :], 0.0, [mbT_t])
    lnd = [sb(es, nc, "lnd%d" % i, [128, 512], F32) for i in range(3)]; lnd_t = [T("lnd%d" % i) for i in range(3)]
    osb = [sb(es, nc, "osb%d" % i, [128, 512], BF16) for i in range(3)]; osb_t = [T("osb%d" % i) for i in range(3)]
    rdb = [sb(es, nc, "rdb%d" % i, [128, 512], BF16) for i in range(3)]; rdb_t = [T("rdb%d" % i) for i in range(3)]
    f2 = sb(es, nc, "f2", [128, 512], F32); f2_t = T("f2")
    facc = sb(es, nc, "facc", [128, 512], F32); facc_t = T("facc")
    yaT = [sb(es, nc, "yaT%d" % i, [128, 4, 128], BF16) for i in range(2)]; yaT_t = [T("yaT%d" % i) for i in range(2)]
    P.memset("pool", hid[0][:, :], 0.0, [hid_t[0]])
    P.memset("pool", hid[1][:, :], 0.0, [hid_t[1]])
    pti = [0]

    for g in range(4):
        plist = [("q", r, C_Q + (4 * g + r) * 128, True, SC) for r in range(4)]
        plist += [("kc", 0, C_KC + g * 128, True, 1.0), ("ks", 0, C_KS + g * 128, True, 1.0),
                  ("kw", 0, C_KW + g * 128, True, 1.0), ("vc", 0, C_VC + g * 128, False, 1.0)]
        for (nm, r, col, rope, scl) in plist:
            wb, wbt = load_w(col, 128)
            for t4 in range(4):
                tsl = slice(t4 * 512, (t4 + 1) * 512)
                pt, ptt = q[t4 % 2], q_t[t4 % 2]
                if nm == "q":
                    dst, dst_t = qT[:, r, tsl], qT_t[t4]
                    dst32 = qT[0:32, r, tsl]
                else:
                    dst, dst_t = kT[nm][0][:, tsl], kT[nm][1][t4]
                    dst32 = kT[nm][0][0:32, tsl]
                for kc in range(16):
                    P.mm(pt[:, :], wb[:, kc, :], hT[:, kc, tsl], kc == 0, kc == 15,
                         [wbt] + hT_t[t4 * 4:t4 * 4 + 4], [ptt])
                P.act(dst, pt[:, :], AF.Copy, [ptt], [dst_t], scale=scl)
                if rope:
                    rp, rpt = q[2 + t4 % 2], q_t[2 + t4 % 2]
                    rt, rtt = rtmp[t4 % 2], rtmp_t[t4 % 2]
                    P.mm(rp[0:32, :], RmT[:, :], dst, True, True, [RmT_t, dst_t], [rpt])
                    P.tt("dve", rt[:, :], rp[0:32, :], sinT[:, tsl], ALU.mult, [rpt, sinT_t], [rtt])
                    P.tt("dve", rp[0:32, :], dst32, cosT[:, tsl], ALU.mult, [dst_t, cosT_t, rpt], [rpt])
                    P.tt("dve", dst32, rt[:, :], rp[0:32, :], ALU.add, [rtt, rpt], [dst_t])
        for nm, col in (("vs", C_VS + g * 128), ("vw", C_VW + g * 128)):
            wb, wbt = load_w(col, 128)
            for tt in range(16):
                pt, ptt = q[tt % 2], q_t[tt % 2]
                for kc in range(16):
                    P.mm(pt[:, 0:128], hT[:, kc, tt * 128:(tt + 1) * 128], wb[:, kc, :], kc == 0, kc == 15,
                         [wbt, hT_t[tt]], [ptt])
                P.copy("act", vtm[nm][0][:, tt, :], pt[:, 0:128], [ptt], [vtm[nm][1][tt]])
        for idx, (w1, w1_t, w2, w2_t, src) in enumerate(((w1k, w1k_t, w2k, w2k_t, "kc"), (w1v, w1v_t, w2v, w2v_t, "vc"))):
            sT, sT_t = kT[src]
            hp, hpt = q[2], q_t[2]
            for l in range(32):
                rhs = bass.AP(sT, l, [[sT.ap().ap[0][0], 128], [16, 127]])
                P.mm(hp[:, 0:127], w1[:, l, :], rhs, l == 0, l == 31, [w1_t] + sT_t, [hpt])
            P.act(hid[idx][:, 0:127], hp[:, 0:127], AF.Silu, [hpt, cst_t], [hid_t[idx]], bias=cst[:, idx:idx + 1])
            op_, opt_ = q[3], q_t[3]
            if idx == 0:
                P.mm(op_[:, 0:128], w2[:, :], hid[0][:, :], True, True, [w2_t, hid_t[0]], [opt_])
                P.copy("act", kcmpT[:, :], op_[:, 0:128], [opt_], [kcmpT_t])
            else:
                P.mm(op_[:, 0:128], hid[1][:, :], w2[:, :], True, True, [w2_t, hid_t[1]], [opt_])
                P.copy("act", vcmp[:, :], op_[:, 0:128], [opt_], [vcmp_t])
        for qt in range(16):
            qs = slice(qt * 128, (qt + 1) * 128)
            rq = qT[:, :, qs]
            rq_t = [qT_t[qt // 4]]
            gbb, gbt = gb[qt % 2], gb_t[qt % 2]
            for b3 in range(3):
                gsrc = bass.AP(sg_scr.tensor, sg_scr.offset + (16 * b3 + 4 * g) * 2048 + qt * 128,
                               [[0, 128], [2048, 4], [1, 128]])
                P.dma("sp", gbb[:, b3, :, :], gsrc, [sg_t] + ([gbt] if b3 else []), [gbt])
            s_, st_ = q[pti[0] % 2], q_t[pti[0] % 2]; pti[0] += 1
            P.mm(s_[:, :], kcmpT[:, :], rq, True, False, [kcmpT_t] + rq_t, [st_])
            P.mm(s_[:, :], ident[:, :], bc_mid(cmk[:, qs], 4), False, True, [ident_t, cmk_t], [st_])
            P.act(Pc[:, :, :], s_[:, :].rearrange("p (r t) -> p r t", r=4), AF.Exp, [st_], [Pc_t])
            P.mm(pa[:, 0:512], vcmp[:, :], Pc[:, :, :], True, True, [vcmp_t, Pc_t], [pa0_t])
            P.mm(pa[:, 512:1024], onesb[:, :], Pc[:, :, :], True, True, [onesb_t, Pc_t], [pa1_t])
            for r in range(4):
                P.mm(q[2][:, r * 33:(r + 1) * 33], Pc[:, r, :], mapa[:, :], True, True, [Pc_t, mapa_t], [q_t[2]])
            ipv = q[2][:, 0:132].rearrange("p (r j) -> p r j", r=4)
            P.ts("dve", smi[:, 0:4], ipv[:, :, 32], 1e-20, None, ALU.max, None, [q_t[2]], [smi_t])
            P.recip(smi[:, 4:8], smi[:, 0:4], [smi_t], [smi_t])
            P.ts("dve", imp[:, 0, :], ipv[:, 0, 0:32], smi[:, 4:5], None, ALU.mult, None, [q_t[2], smi_t], [imp_t])
            for r in range(1, 4):
                P.stt("dve", imp[:, 0, :], ipv[:, r, 0:32], smi[:, 4 + r:5 + r], imp[:, 0, :], ALU.mult, ALU.add,
                      [q_t[2], smi_t, imp_t], [imp_t])
            P.tt("dve", imp[:, 0, :], imp[:, 0, :], Asc[:, qt, :], ALU.mult, [imp_t, Asc_t], [imp_t])
            P.tt("dve", imp[:, 0, :], imp[:, 0, :], Bsc[:, qt, :], ALU.add, [imp_t, Bsc_t], [imp_t])
            P.op("dve", lambda e, o=m8[:, 0, :], i=imp[:, 0, :]: e.max(o, i), [imp_t], [m8_t])
            P.op("dve", lambda e, o=imp[:, 1, :], a=m8[:, 0, :], b=imp[:, 0, :]: e.match_replace(o, a, b, -3.0e38),
                 [imp_t, m8_t], [imp_t])
            P.op("dve", lambda e, o=m8[:, 1, :], i=imp[:, 1, :]: e.max(o, i), [imp_t], [m8_t])
            P.ts("dve", imp[:, 2, :], imp[:, 0, :], m8[:, 1, 7:8], None, ALU.is_ge, None, [imp_t, m8_t], [imp_t])
            P.ts("dve", mbf[:, :], imp[:, 2, :], -1.0, 30000.0, ALU.add, ALU.mult, [imp_t], [mbf_t])
            k0 = max(0, qt - 4)
            tiles_ = [("w", kt) for kt in range(k0, qt + 1)] + [("s", kt) for kt in range(qt + 1)]
            pend = None
            for tl_ in tiles_ + [(None, None)]:
                br, kt = tl_
                cur = None
                if br is not None:
                    ks_ = slice(kt * 128, (kt + 1) * 128)
                    s_, st_ = q[pti[0] % 2], q_t[pti[0] % 2]
                    pb, pbt = PTb[pti[0] % 3], PTb_t[pti[0] % 3]; pti[0] += 1
                    mk_ = None
                    if kt == qt:
                        mk_ = (dmask, dmask_t)
                    elif br == "w" and kt == qt - 4:
                        mk_ = (lmask, lmask_t)
                    if br == "w":
                        P.mm(s_[:, :], kT["kw"][0][:, ks_], rq, True, mk_ is None, [kT["kw"][1][kt // 4]] + rq_t, [st_])
                    else:
                        if kt == 0:
                            P.tr(q[3][0:32, 0:128], mbf[:, :], identf[:, :], [mbf_t, identf_t], [q_t[3]])
                            P.copy("act", mbT[0:32, :, :], bc_mid(q[3][0:32, 0:128], 4), [q_t[3]], [mbT_t])
                        P.mm(s_[:, :], kT["ks"][0][:, ks_], rq, True, False, [kT["ks"][1][kt // 4]] + rq_t, [st_])
                        P.mm(s_[:, :], Esel[:, kt, :], mbT[:, :, :], False, mk_ is None, [Esel_t, mbT_t], [st_])
                    if mk_ is not None:
                        P.mm(s_[:, :], ident[:, :], bc_mid(mk_[0][:, :], 4), False, True, [ident_t, mk_[1]], [st_])
                    P.act(pb[:, :, :], s_[:, :].rearrange("p (r t) -> p r t", r=4), AF.Exp, [st_], [pbt])
                    cur = (br, kt, pb, pbt)
                if pend is not None:
                    pbr, pkt, ppb, ppbt = pend
                    if pbr == "w":
                        P.mm(q[4][:, :], vtm["vw"][0][:, pkt, :], ppb[:, :, :], pkt == k0, pkt == qt, [vtm["vw"][1][pkt], ppbt], [q_t[4]])
                        P.mm(q[5][:, :], onesb[:, :], ppb[:, :, :], pkt == k0, pkt == qt, [onesb_t, ppbt], [q_t[5]])
                    else:
                        P.mm(q[2][:, :], vtm["vs"][0][:, pkt, :], ppb[:, :, :], pkt == 0, pkt == qt, [vtm["vs"][1][pkt], ppbt], [q_t[2]])
                        P.mm(q[3][:, :], onesb[:, :], ppb[:, :, :], pkt == 0, pkt == qt, [onesb_t, ppbt], [q_t[3]])
                pend = cur
            branches = ((0, pa[:, 0:512], pa0_t, pa[:, 512:1024], pa1_t), (1, q[2][:, :], q_t[2], q[3][:, :], q_t[3]),
                        (2, q[4][:, :], q_t[4], q[5][:, :], q_t[5]))
            for (b, O_, O_t, Dn, Dn_t) in branches:
                P.act(lnd[b][:, :], Dn, AF.Ln, [Dn_t], [lnd_t[b]], bias=1e-18)
                P.act(osb[b][:, :], O_, AF.Copy, [O_t], [osb_t[b]])
            for (b, O_, O_t, Dn, Dn_t) in branches:
                P.act(rdb[b][:, :], lnd[b][:, :], AF.Exp, [lnd_t[b]], [rdb_t[b]], scale=-1.0)
            ya, yat = yaT[qt % 2], yaT_t[qt % 2]
            for (b, O_, O_t, Dn, Dn_t) in branches:
                P.tt("dve", rdb[b][:, :], rdb[b][:, :], gbb[:, b, :, :].rearrange("p r t -> p (r t)"), ALU.mult, [rdb_t[b], gbt], [rdb_t[b]])
                if b == 0:
                    P.tt("dve", facc[:, :], osb[b][:, :], rdb[b][:, :], ALU.mult, [osb_t[b], rdb_t[b]], [facc_t])
                else:
                    P.tt("dve", f2[:, :], osb[b][:, :], rdb[b][:, :], ALU.mult, [osb_t[b], rdb_t[b]], [f2_t])
                    if b == 1:
                        P.tt("dve", facc[:, :], facc[:, :], f2[:, :], ALU.add, [facc_t, f2_t], [facc_t])
                    else:
                        P.tt("dve", ya[:, :, :].rearrange("p r t -> p (r t)"), facc[:, :], f2[:, :], ALU.add, [facc_t, f2_t], [yat])
                        P.dma("sp", y_attnT[g * 512:(g + 1) * 512, qs].rearrange("(r p) t -> p r t", p=128), ya[:, :, :],
                              [yat], [yaT_tiles[g][qt]])


def merge_gates(P, nc, es, ps, hT, hT_t, dr, gs_scr, ga_scr, g_tiles):
    pa, pa_t, q, q_t = ps
    w_in = dr["w_in"]
    wgs = [sb(es, nc, "mwgs%d" % i, [128, 16, 128], BF16) for i in range(2)]; wgs_t = [T("mwgs%d" % i) for i in range(2)]
    wga = [sb(es, nc, "mwga%d" % i, [128, 16, 128], BF16) for i in range(2)]; wga_t = [T("mwga%d" % i) for i in range(2)]
    gsb = [sb(es, nc, "gsb%d" % i, [128, 2048], BF16) for i in range(2)]; gsb_t = [T("gsb%d" % i) for i in range(2)]
    gab = [sb(es, nc, "gab%d" % i, [128, 2048], BF16) for i in range(2)]; gab_t = [T("gab%d" % i) for i in range(2)]
    for fc in range(16):
        i = fc % 2
        P.dma("pool", wgs[i][:, :, :], w_in[:, C_GS + fc * 128:C_GS + (fc + 1) * 128].rearrange("(kc p) n -> p kc n", p=128), [], [wgs_t[i]])
        P.dma("pool", wga[i][:, :, :], w_in[:, C_GA + fc * 128:C_GA + (fc + 1) * 128].rearrange("(kc p) n -> p kc n", p=128), [], [wga_t[i]])
        for t4 in range(4):
            tsl = slice(t4 * 512, (t4 + 1) * 512)
            j = t4 % 2
            for kc in range(16):
                P.mm(q[j][:, :], wgs[i][:, kc, :], hT[:, kc, tsl], kc == 0, kc == 15, [wgs_t[i]] + hT_t[t4 * 4:t4 * 4 + 4], [q_t[j]])
            for kc in range(16):
                P.mm(q[2 + j][:, :], wga[i][:, kc, :], hT[:, kc, tsl], kc == 0, kc == 15, [wga_t[i]] + hT_t[t4 * 4:t4 * 4 + 4], [q_t[2 + j]])
            P.act(gsb[i][:, tsl], q[j][:, :], AF.Sigmoid, [q_t[j]], [gsb_t[i]])
            P.act(gab[i][:, tsl], q[2 + j][:, :], AF.Sigmoid, [q_t[2 + j]], [gab_t[i]])
        P.dma("sp", gs_scr[fc * 128:(fc + 1) * 128, :], gsb[i][:, :], [gsb_t[i]], [g_tiles[0][fc]])
        P.dma("sp", ga_scr[fc * 128:(fc + 1) * 128, :], gab[i][:, :], [gab_t[i]], [g_tiles[1][fc]])


def merge_phase(P, nc, es, ps, A, A_t, dr, y_ssmT, ys_tiles, y_attnT, ya_tiles, m_scr, m_tiles, gs_scr, ga_scr, g_tiles):
    pa, pa_t, q, q_t = ps
    yab = sb(es, nc, "yab", [128, 16, 1024], BF16); yab_t = T("yab")
    ws = [sb(es, nc, "mws%d" % i, [128, 32, 128], BF16) for i in range(2)]; ws_t = [T("mws%d" % i) for i in range(2)]
    wa = [sb(es, nc, "mwa%d" % i, [128, 16, 128], BF16) for i in range(2)]; wa_t = [T("mwa%d" % i) for i in range(2)]
    gst = [sb(es, nc, "gst%d" % i, [128, 1024], BF16) for i in range(2)]; gst_t = [T("gst%d" % i) for i in range(2)]
    gat = [sb(es, nc, "gat%d" % i, [128, 1024], BF16) for i in range(2)]; gat_t = [T("gat%d" % i) for i in range(2)]
    m1 = sb(es, nc, "m1", [128, 512], F32); m1_t = T("m1")
    m2 = sb(es, nc, "m2", [128, 512], F32); m2_t = T("m2")
    mo = [sb(es, nc, "mo%d" % i, [128, 512], BF16) for i in range(2)]; mo_t = [T("mo%d" % i) for i in range(2)]
    pstep = A.ap().ap[0][0]
    ysb = bass.AP(A, 0, [[pstep, 128], [1024, 32], [1, 1024]])

    def ysb_k(kc, a, b):
        return bass.AP(A, kc * 1024 + a, [[pstep, 128], [1, b - a]])
    it = 0
    for th in range(2):
        hsl = slice(th * 1024, (th + 1) * 1024)
        for half in range(2):
            P.dma("sp", bass.AP(A, half * 16 * 1024, [[pstep, 128], [1024, 16], [1, 1024]]),
                  y_ssmT[half * 2048:(half + 1) * 2048, hsl].rearrange("(kc p) t -> p kc t", p=128),
                  [ys_tiles[g][c] for g in range(half * 4, half * 4 + 4) for c in range(th * 8, th * 8 + 8)], A_t)
        P.dma("sp", yab[:, :, :], y_attnT[:, hsl].rearrange("(kc p) t -> p kc t", p=128),
              [ya_tiles[g][c] for g in range(4) for c in range(th * 8, th * 8 + 8)], [yab_t])
        for fc in range(16):
            i = fc % 2
            csl = slice(fc * 128, (fc + 1) * 128)
            P.dma("pool", ws[i][:, :, :], dr["w_ssm_branch"][:, csl].rearrange("(kc p) n -> p kc n", p=128), [], [ws_t[i]])
            P.dma("pool", wa[i][:, :, :], dr["w_attn_branch"][:, csl].rearrange("(kc p) n -> p kc n", p=128), [], [wa_t[i]])
            P.dma("sp", gst[i][:, :], gs_scr[csl, hsl], [g_tiles[0][fc]], [gst_t[i]])
            P.dma("sp", gat[i][:, :], ga_scr[csl, hsl], [g_tiles[1][fc]], [gat_t[i]])
            for t2 in range(2):
                j = it % 2; it += 1
                a, b = t2 * 512, (t2 + 1) * 512
                for kc in range(32):
                    P.mm(q[j][:, :], ws[i][:, kc, :], ysb_k(kc, a, b), kc == 0, kc == 31, [ws_t[i]] + A_t, [q_t[j]])
                for kc in range(16):
                    P.mm(q[2 + j][:, :], wa[i][:, kc, :], yab[:, kc, a:b], kc == 0, kc == 15, [wa_t[i], yab_t], [q_t[2 + j]])
                P.tt("dve", m1[:, :], q[j][:, :], gst[i][:, a:b], ALU.mult, [q_t[j], gst_t[i]], [m1_t])
                P.tt("dve", m2[:, :], q[2 + j][:, :], gat[i][:, a:b], ALU.mult, [q_t[2 + j], gat_t[i]], [m2_t])
                P.tt("dve", mo[j][:, :], m1[:, :], m2[:, :], ALU.add, [m1_t, m2_t], [mo_t[j]])
                t4 = th * 2 + t2
                P.dma("sp", m_scr[csl, t4 * 512:(t4 + 1) * 512], mo[j][:, :], [mo_t[j]], [m_tiles[fc][t4]])


def linear_tm(P, nc, es, ps, pfx, A, A_t, W, KC, NW, nft, epilogue, extra=None):
    pa, pa_t, q, q_t = ps
    wb = [sb(es, nc, pfx + "w%d" % i, [128, KC, NW], BF16) for i in range(2)]
    wb_t = [T(pfx + "w%d" % i) for i in range(2)]
    it = 0
    for ft in range(nft):
        i = ft % 2
        nsp = max(1, (KC * NW) // (16 * 512))
        kstep = KC // nsp
        for sp_ in range(nsp):
            ksl = slice(sp_ * kstep, (sp_ + 1) * kstep)
            P.dma("pool", wb[i][:, ksl, :], W[sp_ * kstep * 128:(sp_ + 1) * kstep * 128, ft * NW:(ft + 1) * NW].rearrange("(kc p) n -> p kc n", p=128),
                  [wb_t[i]] if sp_ else [], [wb_t[i]])
        nxt = A(0)
        if extra is not None:
            extra(0, ft, it + 1)
        for tt in range(16):
            pt, ptt = q[it % 2], q_t[it % 2]; it += 1
            afn, atiles = nxt
            if tt + 1 < 16:
                nxt = A(tt + 1)
                if extra is not None:
                    extra(tt + 1, ft, it + 1)
            for kc in range(KC):
                P.mm(pt[:, 0:NW], afn(kc), wb[i][:, kc, :], kc == 0, kc == KC - 1, [wb_t[i]] + atiles, [ptt])
            epilogue(tt, ft, pt[:, 0:NW], ptt, it)


def final_norm(P, nc, es, src_rows, wrow, wrow_t, out_dram, out_tiles, pfx):
    xb = [sb(es, nc, pfx + "xb%d" % i, [128, 2048], F32) for i in range(2)]
    xb_t = [T(pfx + "xb%d" % i) for i in range(2)]
    junk = sb(es, nc, pfx + "junk", [128, 2048], BF16); junk_t = T(pfx + "junk")
    ob = [sb(es, nc, pfx + "ob%d" % i, [128, 2048], F32) for i in range(2)]
    ob_t = [T(pfx + "ob%d" % i) for i in range(2)]
    st = sb(es, nc, pfx + "st", [128, 16, 4], F32)
    st_t = [T(pfx + "st%d" % i) for i in range(16)]
    for tt in range(16):
        i = tt % 2
        ap, rt = src_rows(tt)
        P.dma("sp", xb[i][:, :], ap, rt, [xb_t[i]])
        P.act(junk[:, :], xb[i][:, :], AF.Square, [xb_t[i]], [junk_t, st_t[tt]], accum_out=st[:, tt, 0:1])
        P.act(st[:, tt, 1:2], st[:, tt, 0:1], AF.Sqrt, [st_t[tt]], [st_t[tt]], bias=EPS, scale=1.0 / D)
        P.recip(st[:, tt, 2:3], st[:, tt, 1:2], [st_t[tt]], [st_t[tt]])
        P.stt("dve", ob[i][:, :], xb[i][:, :], st[:, tt, 2:3], wrow[:, :], ALU.mult, ALU.mult,
              [xb_t[i], st_t[tt], wrow_t], [ob_t[i]])
        P.dma("sp", out_dram[tt * 128:(tt + 1) * 128, :], ob[i][:, :], [ob_t[i]], [out_tiles[tt]])


def ffn_up_phase(P, nc, es, ps, C, h2T, h2T_t, dr, act_scr, act_tiles):
    pa, pa_t, q, q_t = ps
    fcw = sb(es, nc, "fcw", [128, 44, 3], F32); fcw_t = T("fcw")
    fcb = sb(es, nc, "fcb", [128, 44], F32); fcb_t = T("fcb")
    P.dma("sp", fcw[:, :, :], dr["ffn_cw"].rearrange("p (j k) -> p j k", k=3), [], [fcw_t])
    P.dma("sp", fcb[:, :], dr["ffn_cb"], [], [fcb_t])
    wg = [sb(es, nc, "fwg%d" % i, [128, 16, 128], BF16) for i in range(2)]; wg_t = [T("fwg%d" % i) for i in range(2)]
    wu = [sb(es, nc, "fwu%d" % i, [128, 16, 128], BF16) for i in range(2)]; wu_t = [T("fwu%d" % i) for i in range(2)]
    raw = [sb(es, nc, "fraw%d" % i, [128, 514], F32) for i in range(2)]; raw_t = [T("fraw%d" % i) for i in range(2)]
    acc = [sb(es, nc, "facc%d" % i, [128, 512], F32) for i in range(2)]; acc_t = [T("facc%d" % i) for i in range(2)]
    sg = [sb(es, nc, "fsg%d" % i, [128, 512], F32) for i in range(2)]; sg_t = [T("fsg%d" % i) for i in range(2)]
    ao = [sb(es, nc, "fao%d" % i, [128, 16, 128], BF16) for i in range(2)]; ao_t = [T("fao%d" % i) for i in range(2)]
    for fc in range(44):
        i = fc % 2
        csl = slice(fc * 128, (fc + 1) * 128)
        P.dma("pool", wg[i][:, :, :], dr["ffn_w_gate"][:, csl].rearrange("(kc p) n -> p kc n", p=128), [], [wg_t[i]])
        P.dma("pool", wu[i][:, :, :], dr["ffn_w_up"][:, csl].rearrange("(kc p) n -> p kc n", p=128), [], [wu_t[i]])
        P.memset("pool", raw[0][:, 0:2], 0.0, [raw_t[0]])
        for t4 in range(4):
            tsl = slice(t4 * 512, (t4 + 1) * 512)
            j = t4 % 2
            for kc in range(16):
                P.mm(q[j][:, :], wg[i][:, kc, :], h2T[:, kc, tsl], kc == 0, kc == 15, [wg_t[i]] + h2T_t[t4 * 4:t4 * 4 + 4], [q_t[j]])
            for kc in range(16):
                P.mm(q[2 + j][:, :], wu[i][:, kc, :], h2T[:, kc, tsl], kc == 0, kc == 15, [wu_t[i]] + h2T_t[t4 * 4:t4 * 4 + 4], [q_t[2 + j]])
            rw, rwt = raw[j], raw_t[j]
            P.copy("act", rw[:, 2:514], q[j][:, :], [q_t[j]], [rwt])
            if t4 < 3:
                P.copy("pool", raw[1 - j][:, 0:2], rw[:, 512:514], [rwt], [raw_t[1 - j]])
            P.ts("dve", acc[j][:, :], rw[:, 2:514], fcw[:, fc, 2:3], fcb[:, fc:fc + 1], ALU.mult, ALU.add, [rwt, fcw_t, fcb_t], [acc_t[j]])
            for k in (1, 0):
                P.stt("dve", acc[j][:, :], rw[:, k:k + 512], fcw[:, fc, k:k + 1], acc[j][:, :], ALU.mult, ALU.add,
                      [rwt, fcw_t, acc_t[j]], [acc_t[j]])
            P.act(sg[j][:, :], acc[j][:, :], AF.Silu, [acc_t[j]], [sg_t[j]])
            P.tt("dve", ao[i][:, t4 * 4:(t4 + 1) * 4, :].rearrange("p a b -> p (a b)"), q[2 + j][:, :], sg[j][:, :], ALU.mult,
                 [q_t[2 + j], sg_t[j]], [ao_t[i]])
        P.dma("sp", act_scr[:, :, fc, :].rearrange("t p c -> p t c"), ao[i][:, :, :], [ao_t[i]], [act_tiles[fc]])

import numpy as np

def consts_np():
    s = np.arange(128)
    c = {}
    c["c_ident"] = np.eye(128, dtype=np.float32)
    c["c_U"] = (s[:, None] <= s[None, :]).astype(np.float32)
    c["c_onesf"] = np.ones((128, 128), np.float32)
    c["c_maskneg"] = np.where(s[None, :] >= s[:, None], 0.0, -30000.0).astype(np.float32)
    R = np.zeros((128, 32), np.float32)
    for m in range(16):
        R[m + 16, m] = -1.0
        R[m, m + 16] = 1.0
    c["c_RmT"] = R
    half = 16
    inv = 500000.0 ** (-(np.arange(half, dtype=np.float32) * 2.0 / 32.0))
    ang = np.arange(2048, dtype=np.float32)[None, :] * inv[:, None].astype(np.float32)
    c["c_cos"] = np.concatenate([np.cos(ang), np.cos(ang)], 0).astype(np.float32)
    c["c_sin"] = np.concatenate([np.sin(ang), np.sin(ang)], 0).astype(np.float32)
    i = np.arange(128)[:, None]; t = np.arange(2048)[None, :]
    c["c_cmask"] = np.where((16 * i + 31 <= t) & (i < 127), 0.0, -30000.0).astype(np.float32)
    tt = np.arange(2048)[:, None]; j = np.arange(32)[None, :]
    cur = tt // 64
    causal = j <= cur
    forced = (j == 0) | (j == cur) | (j == cur - 1)
    c["c_A"] = (causal & ~forced).astype(np.float32)
    c["c_B"] = (forced * 1.0e4 + (~causal) * (-1.0e30)).astype(np.float32)
    E = np.zeros((128, 16, 128), np.float32)
    for kt in range(16):
        for p in range(128):
            E[2 * kt + p // 64, kt, p] = 1.0
    c["c_E"] = E
    c["c_diag"] = np.where(s[:, None] <= s[None, :], 0.0, -30000.0).astype(np.float32)
    c["c_low"] = np.where(s[:, None] > s[None, :], 0.0, -30000.0).astype(np.float32)
    m = np.zeros((128, 33), np.float32)
    ii = np.arange(128)[:, None]; jj = np.arange(32)[None, :]
    m[:, :32] = ((16 * ii < 64 * jj + 64) & (16 * ii + 32 > 64 * jj) & (ii < 127)).astype(np.float32)
    m[:, 32] = 1.0
    c["c_map"] = m
    return c


def build_program():
    nc = bass.Bass("TRN2", target_bir_lowering=False)
    dr = {}

    def din(name, shape):
        dr[name] = nc.dram_tensor(name, list(shape), F32, kind="ExternalInput").ap()

    def dscr(name, shape, dt):
        return nc.dram_tensor(name, list(shape), dt, kind="Internal").ap()

    din("x", [2048, 2048]); din("p", [2048, 256]); din("w_in", [2048, NIN])
    for nm in ("norm_mix_w", "norm_ffn_w", "ple_norm_w", "final_norm_w"):
        din(nm, [1, 2048])
    din("ssm_cw", [128, 192]); din("ssm_cb", [128, 48]); din("ssm_dt_bias", [1, 64]); din("ssm_a_log", [1, 64])
    din("ssm_d", [1, 64]); din("ssm_norm_w", [1, 4096])
    din("cmp_wk1", [4096, 128]); din("cmp_wv1", [4096, 128]); din("cmp_wk2", [128, 128]); din("cmp_wv2", [128, 128])
    din("cmp_pekT", [128, 32]); din("cmp_pevT", [128, 32])
    din("w_ssm_branch", [4096, 2048]); din("w_attn_branch", [2048, 2048]); din("w_mix_out", [2048, 2048])
    din("ffn_w_gate", [2048, DFF]); din("ffn_w_up", [2048, DFF]); din("ffn_w_down", [DFF, 2048])
    din("ffn_cw", [128, 132]); din("ffn_cb", [128, 44])
    din("ple_w_gate", [2048, 2048]); din("ple_w_proj", [256, 2048])
    for k, v in consts_np().items():
        din(k, v.shape)
    out = nc.dram_tensor("out", [2048, 2048], F32, kind="ExternalOutput").ap()
    y_ssmT = dscr("y_ssmT", [4096, 2048], BF16)
    y_attnT = dscr("y_attnT", [2048, 2048], BF16)
    sg_scr = dscr("sg_scr", [48, 2048], BF16)
    ac_scr = dscr("ac_scr", [16, 8, 8, 128], F32)
    m_scr = dscr("m_scr", [2048, 2048], BF16)
    gs_scr = dscr("gs_scr", [2048, 2048], BF16)
    ga_scr = dscr("ga_scr", [2048, 2048], BF16)
    x1_scr = dscr("x1_scr", [2048, 2048], F32)
    x2_scr = dscr("x2_scr", [2048, 2048], F32)
    x3_scr = dscr("x3_scr", [2048, 2048], F32)
    act_scr = dscr("act_scr", [16, 128, 44, 128], BF16)

    P = B(nc)
    with ExitStack() as es:
        pa = es.enter_context(nc.psum_tensor("pa", [128, 1024], F32))
        q = [es.enter_context(nc.psum_tensor("q%d" % i, [128, 512], F32)) for i in range(6)]
        q_t = [T("q%d" % i) for i in range(6)]
        ps = (pa, (T("pa0"), T("pa1")), q, q_t)
        ps01 = [(q[0], q_t[0]), (q[1], q_t[1])]
        C = load_consts(P, nc, es, dr)
        ident, ident_t = C["ident"]
        wrow_t = T("wrow")
        A = sb(es, nc, "Abuf", [128, 16, 2048], BF16); A_t = [T("A%d" % i) for i in range(16)]

        with ExitStack() as es2:
            wrow = sb(es2, nc, "wrow0", [128, 2048], F32)
            P.dma("sp", wrow[:, :], dr["norm_mix_w"].broadcast_to([128, 2048]), [], [wrow_t])
            norm_to_T(P, nc, es2, ps01, ident, lambda tt: (dr["x"][tt * 128:(tt + 1) * 128, :], []), wrow, wrow_t, A, A_t, "n0", ident_t)
        P.barrier()
        ys_tiles = [[T("ys%d_%d" % (g, c)) for c in range(16)] for g in range(8)]
        with ExitStack() as es2:
            ssm_phase(P, nc, es2, ps, C, A, A_t, dr, y_ssmT, ys_tiles, ac_scr)
        P.barrier()
        ya_tiles = [[T("ya%d_%d" % (g, c)) for c in range(16)] for g in range(4)]
        with ExitStack() as es2:
            attn_phase(P, nc, es2, ps, C, A, A_t, dr, y_attnT, ya_tiles, sg_scr)
        P.barrier()
        m_tiles = [[T("m%d_%d" % (fc, t4)) for t4 in range(4)] for fc in range(16)]
        g_tiles = [[T("gs%d" % fc) for fc in range(16)], [T("ga%d" % fc) for fc in range(16)]]
        with ExitStack() as es2:
            merge_gates(P, nc, es2, ps, A, A_t, dr, gs_scr, ga_scr, g_tiles)
        P.barrier()
        with ExitStack() as es2:
            merge_phase(P, nc, es2, ps, A, A_t, dr, y_ssmT, ys_tiles, y_attnT, ya_tiles, m_scr, m_tiles, gs_scr, ga_scr, g_tiles)
        P.barrier()
        x1_tiles = [[T("x1_%d_%d" % (tt, ft)) for ft in range(4)] for tt in range(16)]
        for t4 in range(4):
            tsl = slice(t4 * 512, (t4 + 1) * 512)
            P.dma("sp", A[:, :, tsl], m_scr[:, tsl].rearrange("(kc p) t -> p kc t", p=128),
                  [m_tiles[fc][t4] for fc in range(16)], A_t[t4 * 4:t4 * 4 + 4])
        esG1 = ExitStack()
        for es2 in [esG1]:
            xin = [sb(es2, nc, "p4xin%d" % i, [128, 512], F32) for i in range(3)]; xin_t = [T("xin%d" % i) for i in range(3)]
            xo = [sb(es2, nc, "p4xo%d" % i, [128, 512], F32) for i in range(3)]; xo_t = [T("xo%d" % i) for i in range(3)]

            def pre4(tt, ft, it):
                i = it % 3
                P.dma("sp", xin[i][:, :], dr["x"][tt * 128:(tt + 1) * 128, ft * 512:(ft + 1) * 512], [], [xin_t[i]])

            def epi4(tt, ft, pt, ptt, it):
                i = it % 3
                P.tt("dve", xo[i][:, :], pt, xin[i][:, :], ALU.add, [ptt, xin_t[i]], [xo_t[i]])
                P.dma("sp", x1_scr[tt * 128:(tt + 1) * 128, ft * 512:(ft + 1) * 512], xo[i][:, :], [xo_t[i]], [x1_tiles[tt][ft]])
            linear_tm(P, nc, es2, ps, "l4", lambda tt: ((lambda kc: A[:, kc, tt * 128:(tt + 1) * 128]), [A_t[tt]]), None,
                      dr["w_mix_out"], 16, 512, 4, epi4, extra=pre4)
        wrow = sb(esG1, nc, "wrow1", [128, 2048], F32); wrow_t = T("wrow1")
        P.dma("sp", wrow[:, :], dr["norm_ffn_w"].broadcast_to([128, 2048]), [], [wrow_t])
        for es2 in [esG1]:
            norm_to_T(P, nc, es2, ps01, ident, lambda tt: (x1_scr[tt * 128:(tt + 1) * 128, :], x1_tiles[tt]), wrow, wrow_t, A, A_t, "n1", ident_t)
        act_tiles = [T("act%d" % fc) for fc in range(44)]
        for es2 in [esG1]:
            ffn_up_phase(P, nc, es2, ps, C, A, A_t, dr, act_scr, act_tiles)
        esG1.close()
        P.barrier()
        x2_tiles = [[T("x2_%d_%d" % (tt, ft)) for ft in range(4)] for tt in range(16)]
        with ExitStack() as es2:
            ab = [sb(es2, nc, "ab%d" % i, [128, 44, 128], BF16) for i in range(2)]; ab_t = [T("ab%d" % i) for i in range(2)]
            xin = [sb(es2, nc, "p6xin%d" % i, [128, 512], F32) for i in range(3)]; xin_t = [T("xin%d" % i) for i in range(3)]
            xo = [sb(es2, nc, "p6xo%d" % i, [128, 512], F32) for i in range(3)]; xo_t = [T("xo%d" % i) for i in range(3)]
            cnt = [0]

            def A6(tt):
                i = cnt[0] % 2; cnt[0] += 1
                P.dma("sp", ab[i][:, :, :], act_scr[tt, :, :, :], act_tiles, [ab_t[i]])
                return (lambda kc: ab[i][:, kc, :]), [ab_t[i]]

            def pre6(tt, ft, it):
                i = it % 3
                P.dma("sp", xin[i][:, :], x1_scr[tt * 128:(tt + 1) * 128, ft * 512:(ft + 1) * 512], [x1_tiles[tt][ft]], [xin_t[i]])

            def epi6(tt, ft, pt, ptt, it):
                i = it % 3
                P.tt("dve", xo[i][:, :], pt, xin[i][:, :], ALU.add, [ptt, xin_t[i]], [xo_t[i]])
                P.dma("sp", x2_scr[tt * 128:(tt + 1) * 128, ft * 512:(ft + 1) * 512], xo[i][:, :], [xo_t[i]], [x2_tiles[tt][ft]])
            linear_tm(P, nc, es2, ps, "l6", A6, None, dr["ffn_w_down"], 44, 512, 4, epi6, extra=pre6)
        P.barrier()
        esG2 = ExitStack()
        wrow = sb(esG2, nc, "wrow2", [128, 2048], F32); wrow_t = T("wrow2")
        P.dma("sp", wrow[:, :], dr["ple_norm_w"].broadcast_to([128, 2048]), [], [wrow_t])
        for es2 in [esG2]:
            norm_to_T(P, nc, es2, ps01, ident, lambda tt: (x2_scr[tt * 128:(tt + 1) * 128, :], x2_tiles[tt]), wrow, wrow_t, A, A_t, "n2", ident_t)
        x3_tiles = [[T("x3_%d_%d" % (tt, ft)) for ft in range(4)] for tt in range(16)]
        for es2 in [esG2]:
            pT = sb(es2, nc, "pT", [128, 2, 2048], BF16); pT_t = [T("pT%d" % i) for i in range(16)]
            pin = [sb(es2, nc, "pin%d" % i, [128, 256], F32) for i in range(2)]; pin_t = [T("pin%d" % i) for i in range(2)]
            pbf = [sb(es2, nc, "pbf%d" % i, [128, 256], BF16) for i in range(2)]; pbf_t = [T("pbf%d" % i) for i in range(2)]
            q2b = q[2].bitcast(BF16); q3b = q[3].bitcast(BF16)
            for tt in range(16):
                i = tt % 2
                qb, qbt = (q2b, q_t[2]) if i == 0 else (q3b, q_t[3])
                P.dma("sp", pin[i][:, :], dr["p"][tt * 128:(tt + 1) * 128, :], [], [pin_t[i]])
                P.copy("dve", pbf[i][:, :], pin[i][:, :], [pin_t[i]], [pbf_t[i]])
                for kc in range(2):
                    P.tr(qb[:, kc * 128:(kc + 1) * 128], pbf[i][:, kc * 128:(kc + 1) * 128], ident[:, :], [pbf_t[i], ident_t], [qbt])
                P.copy("act", pT[:, :, tt * 128:(tt + 1) * 128], qb[:, 0:256].rearrange("p (a b) -> p a b", a=2), [qbt], [pT_t[tt]])
            wpp = sb(es2, nc, "wpp", [128, 2, 2048], BF16); wpp_t = T("wpp")
            P.dma("pool", wpp[:, :, :], dr["ple_w_proj"].rearrange("(kc p) n -> p kc n", p=128), [], [wpp_t])
            xin = [sb(es2, nc, "p7xin%d" % i, [128, 512], F32) for i in range(3)]; xin_t = [T("xin%d" % i) for i in range(3)]
            xo = [sb(es2, nc, "p7xo%d" % i, [128, 512], F32) for i in range(3)]; xo_t = [T("xo%d" % i) for i in range(3)]
            sgb = [sb(es2, nc, "sgb%d" % i, [128, 512], F32) for i in range(2)]; sgb_t = [T("sgb%d" % i) for i in range(2)]

            def pre7(tt, ft, it):
                i = it % 3
                P.dma("sp", xin[i][:, :], x2_scr[tt * 128:(tt + 1) * 128, ft * 512:(ft + 1) * 512], [x2_tiles[tt][ft]], [xin_t[i]])

            def epi7(tt, ft, pt, ptt, it):
                i = it % 3; j = it % 2
                pp, ppt = q[2 + j], q_t[2 + j]
                for kc in range(2):
                    P.mm(pp[:, :], pT[:, kc, tt * 128:(tt + 1) * 128], wpp[:, kc, ft * 512:(ft + 1) * 512], kc == 0, kc == 1,
                         [pT_t[tt], wpp_t], [ppt])
                P.act(sgb[j][:, :], pt, AF.Sigmoid, [ptt], [sgb_t[j]])
                P.tt("dve", sgb[j][:, :], sgb[j][:, :], pp[:, :], ALU.mult, [sgb_t[j], ppt], [sgb_t[j]])
                P.tt("dve", xo[i][:, :], sgb[j][:, :], xin[i][:, :], ALU.add, [sgb_t[j], xin_t[i]], [xo_t[i]])
                P.dma("sp", x3_scr[tt * 128:(tt + 1) * 128, ft * 512:(ft + 1) * 512], xo[i][:, :], [xo_t[i]], [x3_tiles[tt][ft]])
            linear_tm(P, nc, es2, ps, "l7", lambda tt: ((lambda kc: A[:, kc, tt * 128:(tt + 1) * 128]), [A_t[tt]]), None,
                      dr["ple_w_gate"], 16, 512, 4, epi7, extra=pre7)
        P.dma("sp", wrow[:, :], dr["final_norm_w"].broadcast_to([128, 2048]), [], [wrow_t])
        out_tiles = [T("out%d" % i) for i in range(16)]
        for es2 in [esG2]:
            final_norm(P, nc, es2, lambda tt: (x3_scr[tt * 128:(tt + 1) * 128, :], x3_tiles[tt]), wrow, wrow_t, out, out_tiles, "fn")
        esG2.close()
        P.op("sp", None, out_tiles, [])
        P.emit()
    return nc


def _prep_shared(inp):
    f = lambda a: np.ascontiguousarray(np.asarray(a, dtype=np.float32))
    sh = {}
    sh["w_in"] = f(inp["w_in"][0])
    for nm in ("norm_mix_w", "norm_ffn_w", "ple_norm_w"):
        sh[nm] = f(inp[nm][0][None, :])
    sh["final_norm_w"] = f(inp["final_norm_w"][None, :])
    sh["ssm_cw"] = f(inp["ssm_conv_w"][0].T.reshape(48, 128, 4).transpose(1, 0, 2).reshape(128, 192))
    sh["ssm_cb"] = f(inp["ssm_conv_b"][0].reshape(48, 128).T)
    for nm in ("ssm_dt_bias", "ssm_a_log", "ssm_d", "ssm_norm_w"):
        sh[nm] = f(inp[nm][0][None, :])
    for nm in ("cmp_wk1", "cmp_wv1", "cmp_wk2", "cmp_wv2", "w_ssm_branch", "w_attn_branch", "w_mix_out",
               "ffn_w_gate", "ffn_w_up", "ffn_w_down", "ple_w_gate", "ple_w_proj"):
        sh[nm] = f(inp[nm][0])
    sh["cmp_pekT"] = f(inp["cmp_pe_k"][0].T)
    sh["cmp_pevT"] = f(inp["cmp_pe_v"][0].T)
    sh["ffn_cw"] = f(inp["ffn_conv_w"][0].T.reshape(44, 128, 3).transpose(1, 0, 2).reshape(128, 132))
    sh["ffn_cb"] = f(inp["ffn_conv_b"][0].reshape(44, 128).T)
    sh.update(consts_np())
    return sh


def kernel(**inp):
    nc = build_program()
    sh = _prep_shared(inp)
    x = np.asarray(inp["x"], dtype=np.float32)
    p = np.asarray(inp["p"], dtype=np.float32)
    in_maps = []
    for b in range(8):
        m = dict(sh)
        m["x"] = np.ascontiguousarray(x[b])
        m["p"] = np.ascontiguousarray(p[0, b])
        in_maps.append(m)
    res = run_bass_kernel_spmd(nc, in_maps, core_ids=list(range(8)))
    return np.stack([np.asarray(res.results[b]["out"], dtype=np.float32) for b in range(8)], axis=0)
```

```python
from concourse.bass_utils import run_bass_kernel_spmd

from collections import defaultdict
from contextlib import ExitStack

import numpy as np
import concourse.bass as bass
import concourse.mybir as mybir

F32 = mybir.dt.float32
BF16 = mybir.dt.bfloat16
ALU = mybir.AluOpType
AF = mybir.ActivationFunctionType
AX = mybir.AxisListType

RING = 8


class Tile:
    __slots__ = ("name", "w", "r")

    def __init__(self, name):
        self.name = name
        self.w = None
        self.r = []


class Op:
    __slots__ = ("eng", "fn", "reads", "writes", "dma", "sigeng", "seq", "waits",
                 "signals", "clock", "sigval", "tag")

    def __init__(self, eng, fn, reads, writes, dma, tag=""):
        self.eng = eng
        self.fn = fn
        self.reads = reads
        self.writes = writes
        self.dma = dma
        self.waits = []
        self.signals = False
        self.sigval = 0
        self.tag = tag


class Prog:
    ENGS = ("pe", "act", "dve", "pool", "sp")

    def __init__(self, nc):
        self.nc = nc
        self.ops = []

    def op(self, eng, fn, reads=(), writes=(), dma=False, tag=""):
        o = Op(eng, fn, tuple(reads), tuple(writes), dma, tag)
        self.ops.append(o)
        return o

    def barrier(self):
        self.ops.append("BARRIER")

    def analyze(self):
        seq = defaultdict(int)
        known = defaultdict(dict)
        ring_cnt = defaultdict(int)
        ring_last = {}
        last_of = {}
        real_ops = []
        for op in self.ops:
            if isinstance(op, str):
                frontier = list(last_of.values())
                for E in self.ENGS:
                    b = Op(E, None, (), (), False, "barrier")
                    b.sigeng = E
                    seq[E] += 1
                    b.seq = seq[E]
                    self._resolve(b, frontier, known)
                    real_ops.append(b)
                continue
            E = op.eng
            if op.dma:
                k = ring_cnt[E] % RING
                ring_cnt[E] += 1
                op.sigeng = "%s.d%d" % (E, k)
            else:
                op.sigeng = E
            seq[op.sigeng] += 1
            op.seq = seq[op.sigeng]
            deps = []
            for t in op.reads:
                if t.w is not None:
                    deps.append(t.w)
            for t in op.writes:
                if t.w is not None:
                    deps.append(t.w)
                deps.extend(t.r)
            if op.dma:
                if op.sigeng in ring_last:
                    deps.append(ring_last[op.sigeng])
                ring_last[op.sigeng] = op
            self._resolve(op, deps, known)
            if op.dma:
                op.signals = True
            for t in op.reads:
                t.r.append(op)
            for t in op.writes:
                t.w = op
                t.r = []
            last_of[op.sigeng] = op
            real_ops.append(op)
        self.real_ops = real_ops
        cnt = defaultdict(int)
        for op in real_ops:
            if op.signals:
                cnt[op.sigeng] += 1
                op.sigval = cnt[op.sigeng] * (16 if op.dma else 1)
        self.sigengs = sorted(set(o.sigeng for o in real_ops if o.signals))

    @staticmethod
    def _resolve(op, deps, known):
        E = op.eng
        kn = known[E]
        need = {}
        for d in deps:
            if d is op:
                continue
            F = d.sigeng
            if F == "pe" and E == "pe" and not op.dma and not d.dma:
                continue
            if kn.get(F, 0) >= d.seq:
                continue
            if F not in need or need[F].seq < d.seq:
                need[F] = d
        for F, d in need.items():
            op.waits.append(d)
            d.signals = True
            for G, v in d.clock.items():
                if kn.get(G, 0) < v:
                    kn[G] = v
        op.clock = dict(kn)
        op.clock[op.sigeng] = op.seq

    def emit(self):
        nc = self.nc
        self.analyze()
        with ExitStack() as es:
            sems = {}
            for se in self.sigengs:
                sems[se] = es.enter_context(nc.semaphore("s_" + se.replace(".", "_")))
            block = es.enter_context(nc.Block())
            by_eng = defaultdict(list)
            for op in self.real_ops:
                by_eng[op.eng].append(op)

            def runner(ops):
                def run(e):
                    for op in ops:
                        for d in op.waits:
                            e.wait_ge(sems[d.sigeng], d.sigval)
                        if op.fn is None:
                            continue
                        ins = op.fn(e)
                        if op.signals:
                            ins.then_inc(sems[op.sigeng], 16 if op.dma else 1)
                return run

            if by_eng["pe"]:
                block.tensor(runner(by_eng["pe"]))
            if by_eng["act"]:
                block.scalar(runner(by_eng["act"]))
            if by_eng["dve"]:
                block.vector(runner(by_eng["dve"]))
            if by_eng["pool"]:
                block.gpsimd(runner(by_eng["pool"]))
            if by_eng["sp"]:
                block.sync(runner(by_eng["sp"]))

    def stats(self):
        c = defaultdict(int)
        w = defaultdict(int)
        for op in self.real_ops:
            c[op.eng] += 1
            w[op.eng] += len(op.waits)
        return dict(c), dict(w)


D = 2048; S = 2048; NIN = 19568
C_Z = 0; C_XBC = 4096; C_DT = 10240; C_Q = 10304; C_KC = 12352; C_VC = 12864
C_KS = 13376; C_VS = 13888; C_KW = 14400; C_VW = 14912; C_GN = 15424; C_GS = 15472; C_GA = 17520
DFF = 5632
EPS = 1e-6


def bc_mid(a, n):
    return bass.AP(a.tensor, a.offset, [list(a.ap[0]), [0, n]] + [list(x) for x in a.ap[1:]])


def bc_last(a, n):
    return bass.AP(a.tensor, a.offset, [list(x) for x in a.ap] + [[0, n]])


class B(Prog):
    def mm(self, out, lhsT, rhs, start, stop, reads, writes):
        return self.op("pe", lambda e: e.matmul(out, lhsT, rhs, start=start, stop=stop), reads, writes)

    def tr(self, out, in_, ident, reads, writes):
        return self.op("pe", lambda e: e.transpose(out, in_, ident), reads, writes)

    def act(self, out, in_, func, reads, writes, bias=0.0, scale=1.0, accum_out=None):
        if accum_out is None:
            return self.op("act", lambda e: e.activation(out, in_, func, bias=bias, scale=scale), reads, writes)
        return self.op("act", lambda e: e.activation(out, in_, func, bias=bias, scale=scale, accum_out=accum_out), reads, writes)

    def tt(self, eng, out, in0, in1, op, reads, writes):
        return self.op(eng, lambda e: e.tensor_tensor(out, in0, in1, op), reads, writes)

    def ts(self, eng, out, in0, s1, s2, op0, op1, reads, writes):
        if s2 is None:
            return self.op(eng, lambda e: e.tensor_scalar(out, in0, s1, None, op0), reads, writes)
        return self.op(eng, lambda e: e.tensor_scalar(out, in0, s1, s2, op0, op1), reads, writes)

    def stt(self, eng, out, in0, scalar, in1, op0, op1, reads, writes):
        return self.op(eng, lambda e: e.scalar_tensor_tensor(out, in0, scalar, in1, op0, op1), reads, writes)

    def copy(self, eng, out, in_, reads, writes):
        if eng == "act":
            return self.op(eng, lambda e: e.copy(out, in_), reads, writes)
        return self.op(eng, lambda e: e.tensor_copy(out, in_), reads, writes)

    def recip(self, out, in_, reads, writes):
        return self.op("dve", lambda e: e.reciprocal(out, in_), reads, writes)

    def dma(self, eng, out, in_, reads, writes):
        return self.op(eng, lambda e: e.dma_start(out=out, in_=in_), reads, writes, dma=True)

    def memset(self, eng, ap, val, writes):
        return self.op(eng, lambda e: e.memset(ap, val), (), writes)


def T(name):
    return Tile(name)


def sb(es, nc, name, shape, dt):
    return es.enter_context(nc.sbuf_tensor(name, list(shape), dt))


def norm_to_T(P, nc, es, ps, ident, src_rows, wrow, wrow_t, dstT, dstT_tiles, pfx, ident_t):
    xb = [sb(es, nc, pfx + "xb%d" % i, [128, 2048], F32) for i in range(2)]
    xb_t = [T(pfx + "xb%d" % i) for i in range(2)]
    junk = sb(es, nc, pfx + "junk", [128, 2048], BF16); junk_t = T("junk")
    xn = [sb(es, nc, pfx + "xn%d" % i, [128, 2048], BF16) for i in range(2)]
    xn_t = [T(pfx + "xn%d" % i) for i in range(2)]
    st = sb(es, nc, pfx + "st", [128, 16, 4], F32)
    st_t = [T(pfx + "st%d" % i) for i in range(16)]
    psb = [ps[0][0].bitcast(BF16), ps[1][0].bitcast(BF16)]
    for tt in range(16):
        i = tt % 2
        ap, rt = src_rows(tt)
        P.dma("sp", xb[i][:, :], ap, rt, [xb_t[i]])
        P.act(junk[:, :], xb[i][:, :], AF.Square, [xb_t[i]], [junk_t, st_t[tt]], accum_out=st[:, tt, 0:1])
        P.act(st[:, tt, 1:2], st[:, tt, 0:1], AF.Sqrt, [st_t[tt]], [st_t[tt]], bias=EPS, scale=1.0 / D)
        P.recip(st[:, tt, 2:3], st[:, tt, 1:2], [st_t[tt]], [st_t[tt]])
        P.stt("dve", xn[i][:, :], xb[i][:, :], st[:, tt, 2:3], wrow[:, :], ALU.mult, ALU.mult,
              [xb_t[i], st_t[tt], wrow_t], [xn_t[i]])
        for half in range(2):
            pt = psb[half]
            for j in range(8):
                kc = half * 8 + j
                P.tr(pt[:, j * 128:(j + 1) * 128], xn[i][:, kc * 128:(kc + 1) * 128], ident[:, :],
                     [xn_t[i], ident_t], [ps[half][1]])
            src = pt[:, :].rearrange("p (a b) -> p a b", a=8)
            eng = "act" if half == 0 else "dve"
            P.copy(eng, dstT[:, half * 8:half * 8 + 8, tt * 128:(tt + 1) * 128], src,
                   [ps[half][1]], [dstT_tiles[tt]])


def load_consts(P, nc, es, cin):
    C = {}
    def ld(name, shape, dt, src, eng):
        t = sb(es, nc, "k_" + name, shape, dt); tl = T("k_" + name)
        P.dma(eng, t.ap() if False else t[tuple(slice(None) for _ in shape)], src, [], [tl])
        C[name] = (t, tl)
    ld("ident", [128, 128], BF16, cin["c_ident"], "pool")
    ld("U", [128, 128], F32, cin["c_U"], "sp")
    ld("onesf", [128, 128], F32, cin["c_onesf"], "sp")
    ld("maskneg", [128, 128], F32, cin["c_maskneg"], "sp")
    ld("identf", [128, 128], F32, cin["c_ident"], "sp")
    return C


def ssm_phase(P, nc, es, ps, C, hT, hT_t, dr, y_ssmT, ysT_tiles, ac_scr):
    pa, pa_t, q, q_t = ps
    w_in = dr["w_in"]
    ident, ident_t = C["ident"]; U, U_t = C["U"]; onesf, onesf_t = C["onesf"]; mneg, mneg_t = C["maskneg"]
    cw = sb(es, nc, "cw", [128, 48, 4], F32); cw_t = T("cw")
    cb = sb(es, nc, "cb", [128, 48, 1], F32); cb_t = T("cb")
    P.dma("sp", cw[:, :, :], dr["ssm_cw"].rearrange("p (j k) -> p j k", k=4), [], [cw_t])
    P.dma("sp", cb[:, :, :], dr["ssm_cb"].rearrange("p (j k) -> p j k", k=1), [], [cb_t])
    rows = sb(es, nc, "rows", [128, 4, 64], F32); rows_t = T("rows")
    P.dma("sp", rows[:, 0, :], dr["ssm_dt_bias"].broadcast_to([128, 64]), [], [rows_t])
    P.dma("sp", rows[:, 1, :], dr["ssm_a_log"].broadcast_to([128, 64]), [rows_t], [rows_t])
    P.dma("sp", rows[:, 2, :], dr["ssm_d"].broadcast_to([128, 64]), [rows_t], [rows_t])
    P.act(rows[:, 3, :], rows[:, 1, :], AF.Exp, [rows_t], [rows_t])
    P.ts("dve", rows[:, 3, :], rows[:, 3, :], -1.0, None, ALU.mult, None, [rows_t], [rows_t])
    dtb = rows[:, 0, :]; nega = rows[:, 3, :]; dsk = rows[:, 2, :]
    epsb = sb(es, nc, "epsb", [128, 1], F32)
    P.memset("dve", epsb[:, :], EPS, [rows_t])

    dt_all = sb(es, nc, "dt_all", [128, 16, 64], F32)
    eac_all = sb(es, nc, "eac_all", [128, 16, 64], F32)
    nacum_all = sb(es, nc, "nacum_all", [128, 16, 64], F32)
    es_p = ExitStack()
    acum_all = sb(es_p, nc, "acum_all", [128, 16, 64], F32)
    adt_all = sb(es_p, nc, "adt_all", [128, 16, 64], F32)
    acT = sb(es_p, nc, "acT", [64, 2048], F32); acT_t = T("acT")
    identf, identf_t = C["identf"]
    dts_t = [T("dts%d" % i) for i in range(16)]
    wdt = sb(es_p, nc, "wdt", [128, 16, 64], BF16); wdt_t = T("wdt")
    P.dma("pool", wdt[:, :, :], w_in[:, C_DT:C_DT + 64].rearrange("(kc p) n -> p kc n", p=128), [], [wdt_t])
    tmp64 = sb(es_p, nc, "tmp64", [128, 2, 64], F32); tmp64_t = T("tmp64")
    for tt in range(16):
        pt, ptt = q[tt % 2], q_t[tt % 2]
        for kc in range(16):
            P.mm(pt[:, 0:64], hT[:, kc, tt * 128:(tt + 1) * 128], wdt[:, kc, :], kc == 0, kc == 15,
                 [hT_t[tt], wdt_t], [ptt])
        P.tt("dve", tmp64[:, 0, :], pt[:, 0:64], dtb, ALU.add, [ptt, rows_t], [tmp64_t])
        P.act(tmp64[:, 1, :], tmp64[:, 0, :], AF.Exp, [tmp64_t], [tmp64_t])
        P.act(dt_all[:, tt, :], tmp64[:, 1, :], AF.Ln, [tmp64_t], [dts_t[tt]], bias=1.0)
        P.tt("dve", adt_all[:, tt, :], dt_all[:, tt, :], nega, ALU.mult, [dts_t[tt], rows_t], [dts_t[tt]])
        P.mm(q[2][:, 0:64], U[:, :], adt_all[:, tt, :], True, True, [U_t, dts_t[tt]], [q_t[2]])
        P.copy("dve", acum_all[:, tt, :], q[2][:, 0:64], [q_t[2]], [dts_t[tt]])
        P.act(eac_all[:, tt, :], acum_all[:, tt, :], AF.Exp, [dts_t[tt]], [dts_t[tt]])
        P.ts("dve", nacum_all[:, tt, :], acum_all[:, tt, :], -1.0, None, ALU.mult, None, [dts_t[tt]], [dts_t[tt]])
        P.tr(q[3][0:64, 0:128], acum_all[:, tt, :], identf[:, :], [dts_t[tt], identf_t], [q_t[3]])
        P.copy("act", acT[:, tt * 128:(tt + 1) * 128], q[3][0:64, 0:128], [q_t[3]], [acT_t])
    ac_t = T("ac_scr")
    P.dma("sp", ac_scr.rearrange("c g r l -> (g r) c l"), acT[:, :].rearrange("p (c l) -> p c l", c=16), [acT_t], [ac_t])
    es_p.close()
    P.barrier()

    wz = sb(es, nc, "wz", [128, 16, 512], BF16); wz_t = T("wz")
    nwg = sb(es, nc, "nwg", [128, 512], F32); nwg_t = T("nwg")
    wch = [sb(es, nc, "wch%d" % i, [128, 16, 128], BF16) for i in range(2)]
    wch_t = [T("wch%d" % i) for i in range(2)]
    xsT = sb(es, nc, "xsT", [128, 4, 2048], BF16); xsT_t = [T("xsT%d" % i) for i in range(4)]
    bmT = sb(es, nc, "bmT", [128, 2048], BF16); bmT_t = [T("bmT%d" % i) for i in range(4)]
    cmT = sb(es, nc, "cmT", [128, 2048], BF16); cmT_t = [T("cmT%d" % i) for i in range(4)]
    raw = [sb(es, nc, "raw%d" % i, [128, 515], F32) for i in range(2)]; raw_t = [T("raw%d" % i) for i in range(2)]
    acc = [sb(es, nc, "acc%d" % i, [128, 512], F32) for i in range(2)]; acc_t = [T("acc%d" % i) for i in range(2)]
    arows = [sb(es, nc, "arow%d" % i, [128, 8, 128], F32) for i in range(3)]; arows_t = [T("arow%d" % i) for i in range(3)]
    segb = sb(es, nc, "segb", [128, 8, 128], F32); segb_t = T("segb")
    decs = [sb(es, nc, "dec%d" % i, [128, 8, 128], BF16) for i in range(2)]; decs_t = [T("dec%d" % i) for i in range(2)]
    szall = sb(es, nc, "szall", [128, 16, 512], BF16); szall_t = [T("szall%d" % i) for i in range(16)]
    Dg = sb(es, nc, "Dg", [128, 8, 128], BF16); Dg_t = T("Dg")
    cbs = [sb(es, nc, "cbs%d" % i, [128, 128], BF16) for i in range(2)]; cbs_t = [T("cbs%d" % i) for i in range(2)]
    xss = [sb(es, nc, "xss%d" % i, [128, 8, 64], BF16) for i in range(3)]; xss_t = [T("xss%d" % i) for i in range(3)]
    bms = [sb(es, nc, "bms%d" % i, [128, 128], BF16) for i in range(2)]; bms_t = [T("bms%d" % i) for i in range(2)]
    sms = [sb(es, nc, "sms%d" % i, [128, 4, 8], F32) for i in range(2)]; sms_t = [T("sms%d" % i) for i in range(2)]
    t1s = [sb(es, nc, "t1s%d" % i, [128, 8, 64], F32) for i in range(2)]; t1s_t = [T("t1s%d" % i) for i in range(2)]
    MT = sb(es, nc, "MT", [128, 8, 128], BF16); MT_t = T("MT")
    x_bf = sb(es, nc, "x_bf", [128, 8, 64], BF16); x_bf_t = T("x_bf")
    xd = sb(es, nc, "xd", [128, 8, 64], BF16); xd_t = T("xd")
    yg = sb(es, nc, "yg", [128, 512], F32); yg_t = T("yg")
    junk = sb(es, nc, "sjunk", [128, 512], BF16); junk_t = T("sjunk")
    st = sb(es, nc, "sst", [128, 4], F32); st_t = T("sst")
    yn = sb(es, nc, "yn", [128, 512], BF16); yn_t = T("yn")
    yTs = [sb(es, nc, "yTs%d" % i, [128, 4, 128], BF16) for i in range(2)]; yTs_t = [T("yTs%d" % i) for i in range(2)]
    hst = sb(es, nc, "hst", [128, 8, 64], F32); hst_t = T("hst")
    hpb = sb(es, nc, "hpb", [128, 512], BF16); hpb_t = T("hpb")
    q3b = q[3].bitcast(BF16)
    q2b = q[2].bitcast(BF16)
    q4b = q[4].bitcast(BF16)
    q3x_t = T("q3x"); q3y_t = T("q3y"); q2c_t = q_t[2]; q2b_t = T("q2b")
    pa0_t, pa1_t = pa_t
    wi = 0
    for g in range(8):
        P.dma("pool", wz[:, :, :], w_in[:, C_Z + g * 512:C_Z + (g + 1) * 512].rearrange("(kc p) n -> p kc n", p=128), [], [wz_t])
        P.dma("sp", nwg[:, :], dr["ssm_norm_w"][:, g * 512:(g + 1) * 512].broadcast_to([128, 512]), [], [nwg_t])
        chunks = [(C_XBC + g * 512 + j * 128, g * 4 + j, ("xs", j)) for j in range(4)]
        chunks.append((C_XBC + 4096 + g * 128, 32 + g, ("bm", 0)))
        chunks.append((C_XBC + 5120 + g * 128, 40 + g, ("cm", 0)))
        for (col, ch, (kind, j)) in chunks:
            wb, wbt = wch[wi % 2], wch_t[wi % 2]; wi += 1
            P.dma("pool", wb[:, :, :], w_in[:, col:col + 128].rearrange("(kc p) n -> p kc n", p=128), [], [wbt])
            P.memset("dve", raw[0][:, 0:3], 0.0, [raw_t[0]])
            for t4 in range(4):
                pt, ptt = q[t4 % 2], q_t[t4 % 2]
                rw, rwt = raw[t4 % 2], raw_t[t4 % 2]
                ac, act_ = acc[t4 % 2], acc_t[t4 % 2]
                for kc in range(16):
                    P.mm(pt[:, :], wb[:, kc, :], hT[:, kc, t4 * 512:(t4 + 1) * 512], kc == 0, kc == 15,
                         [wbt] + hT_t[t4 * 4:t4 * 4 + 4], [ptt])
                P.copy("act", rw[:, 3:515], pt[:, :], [ptt], [rwt])
                if t4 < 3:
                    P.copy("act", raw[(t4 + 1) % 2][:, 0:3], rw[:, 512:515], [rwt], [raw_t[(t4 + 1) % 2]])
                P.ts("dve", ac[:, :], rw[:, 3:515], cw[:, ch, 3:4], cb[:, ch, 0:1], ALU.mult, ALU.add,
                     [rwt, cw_t, cb_t], [act_])
                for k in (2, 1, 0):
                    P.stt("dve", ac[:, :], rw[:, k:k + 512], cw[:, ch, k:k + 1], ac[:, :], ALU.mult, ALU.add,
                          [rwt, cw_t, act_], [act_])
                if kind == "xs":
                    dst, dt_ = xsT[:, j, t4 * 512:(t4 + 1) * 512], xsT_t[t4]
                elif kind == "bm":
                    dst, dt_ = bmT[:, t4 * 512:(t4 + 1) * 512], bmT_t[t4]
                else:
                    dst, dt_ = cmT[:, t4 * 512:(t4 + 1) * 512], cmT_t[t4]
                P.act(dst, ac[:, :], AF.Silu, [act_], [dt_])
        for c in range(16):
            zp, zpt = q[c % 2], q_t[c % 2]
            for kc in range(16):
                P.mm(zp[:, :], hT[:, kc, c * 128:(c + 1) * 128], wz[:, kc, :], kc == 0, kc == 15, [hT_t[c], wz_t], [zpt])
            P.act(szall[:, c, :], zp[:, :], AF.Silu, [zpt], [szall_t[c]])
        for r in range(8):
            P.ts("dve", Dg[:, r, :], ident[:, :], dsk[:, g * 8 + r:g * 8 + r + 1], None, ALU.mult, None, [ident_t, rows_t], [Dg_t])
        P.memset("dve", hst[:, :, :], 0.0, [hst_t])
        P.memset("dve", hpb[:, :], 0.0, [hpb_t])
        hs = slice(g * 8, (g + 1) * 8)

        def S1(c):
            t4 = c // 4
            tk = slice(c * 128, (c + 1) * 128)
            ar_, art_ = arows[c % 3], arows_t[c % 3]
            sm_, smt_ = sms[c % 2], sms_t[c % 2]
            for cc in ([0, 1, 2] if c == 0 else ([c + 2] if c + 2 < 16 else [])):
                asrc = bass.AP(ac_scr.tensor, ac_scr.offset + ((cc * 8 + g) * 8) * 128, [[0, 128], [1, 1024]])
                P.dma("sp", arows[cc % 3][:, :, :].rearrange("p r l -> p (r l)"), asrc, [ac_t], [arows_t[cc % 3]])
            P.mm(q[2][:, 0:128], bmT[:, tk], cmT[:, tk], True, True, [bmT_t[t4], cmT_t[t4]], [q2c_t])
            P.tt("dve", cbs[c % 2][:, :], q[2][:, 0:128], U[:, :], ALU.mult, [q2c_t, U_t], [cbs_t[c % 2]])
            P.tt("pool", segb[:, :, :], ar_[:, :, :], bc_last(nacum_all[:, c, hs], 128), ALU.add, [art_, dts_t[c]], [segb_t])
            P.act(sm_[:, 2, :], ar_[:, :, 127], AF.Exp, [art_], [smt_])
            P.act(segb[:, :, :], segb[:, :, :], AF.Relu, [segb_t], [segb_t], scale=-1.0)
            P.act(decs[c % 2][:, :, :], segb[:, :, :], AF.Exp, [segb_t], [decs_t[c % 2]], scale=-1.0)
            for j in range(4):
                P.tr(q3b[:, j * 128:(j + 1) * 128], xsT[:, j, tk], ident[:, :], [xsT_t[t4], ident_t], [q3x_t])
            P.tr(q3b[:, 512:640], bmT[:, tk], ident[:, :], [bmT_t[t4], ident_t], [q3x_t])
            P.copy("act", xss[c % 3][:, :, :], q3b[:, 0:512].rearrange("p (r d) -> p r d", r=8), [q3x_t], [xss_t[c % 3]])
            P.copy("act", bms[c % 2][:, :], q3b[:, 512:640], [q3x_t], [bms_t[c % 2]])

        def S2a(c):
            t4 = c // 4
            tk = slice(c * 128, (c + 1) * 128)
            dtg = dt_all[:, c, hs]
            sm_, smt_ = sms[c % 2], sms_t[c % 2]
            xs_, xst_ = xss[c % 3], xss_t[c % 3]
            P.tt("dve", MT[:, :, :], decs[c % 2][:, :, :], bc_mid(cbs[c % 2][:, :], 8), ALU.mult,
                 [decs_t[c % 2], cbs_t[c % 2]], [MT_t])
            P.tt("dve", x_bf[:, :, :], xs_[:, :, :], bc_last(dtg, 64), ALU.mult, [xst_, dts_t[c]], [x_bf_t])
            P.tt("dve", xd[:, :, :], x_bf[:, :, :], bc_last(decs[c % 2][:, :, 127], 64), ALU.mult, [x_bf_t, decs_t[c % 2]], [xd_t])
            for r in range(8):
                P.mm(q[4][:, r * 64:(r + 1) * 64], MT[:, r, :], x_bf[:, r, :], True, False, [MT_t, x_bf_t], [q_t[4]])
                P.mm(q[4][:, r * 64:(r + 1) * 64], Dg[:, r, :], xs_[:, r, :], False, True, [Dg_t, xst_], [q_t[4]])
            P.mm(q[5][:, :], cmT[:, tk], hpb[:, :], True, True, [cmT_t[t4], hpb_t], [q_t[5]])
            P.mm(q[1][:, :], bms[c % 2][:, :], xd[:, :, :], True, True, [bms_t[c % 2], xd_t], [q_t[1]])

        def S2b(c):
            eag = eac_all[:, c, hs]
            sm_, smt_ = sms[c % 2], sms_t[c % 2]
            t1_, t1t_ = t1s[c % 2], t1s_t[c % 2]
            P.tt("dve", t1_[:, :, :], q[5][:, :].rearrange("p (r d) -> p r d", r=8), bc_last(eag, 64), ALU.mult,
                 [q_t[5], dts_t[c]], [t1t_])
            P.tt("dve", t1_[:, :, :], t1_[:, :, :], q[4][:, :].rearrange("p (r d) -> p r d", r=8), ALU.add, [t1t_, q_t[4]], [t1t_])
            P.tt("dve", hst[:, :, :], hst[:, :, :], bc_last(sm_[:, 2, :], 64), ALU.mult, [hst_t, smt_], [hst_t])
            P.tt("dve", hst[:, :, :], hst[:, :, :], q[1][:, :].rearrange("p (r d) -> p r d", r=8), ALU.add, [hst_t, q_t[1]], [hst_t])
            P.copy("act", hpb[:, :], hst[:, :, :].rearrange("p r d -> p (r d)"), [hst_t], [hpb_t])

        def S3(c):
            tk = slice(c * 128, (c + 1) * 128)
            xs_, xst_ = xss[c % 3], xss_t[c % 3]
            t1_, t1t_ = t1s[c % 2], t1s_t[c % 2]
            P.tt("dve", yg[:, :], t1_[:, :, :].rearrange("p r d -> p (r d)"), szall[:, c, :], ALU.mult, [t1t_, szall_t[c]], [yg_t])
            P.act(junk[:, :], yg[:, :], AF.Square, [yg_t], [junk_t, st_t], accum_out=st[:, 0:1])
            P.act(st[:, 1:2], st[:, 0:1], AF.Ln, [st_t], [st_t], bias=epsb[:, 0:1], scale=1.0 / 512)
            P.act(st[:, 2:3], st[:, 1:2], AF.Exp, [st_t], [st_t], scale=-0.5)
            P.stt("dve", yn[:, :], yg[:, :], st[:, 2:3], nwg[:, :], ALU.mult, ALU.mult, [yg_t, st_t, nwg_t], [yn_t])
            for j in range(4):
                P.tr(q3b[:, j * 128:(j + 1) * 128], yn[:, j * 128:(j + 1) * 128], ident[:, :], [yn_t, ident_t], [q3x_t])
            ys, yst = yTs[c % 2], yTs_t[c % 2]
            P.copy("act", ys[:, :, :], q3b[:, 0:512].rearrange("p (j t) -> p j t", j=4), [q3x_t], [yst])
            P.dma("sp", y_ssmT[g * 512:(g + 1) * 512, tk].rearrange("(j p) t -> p j t", p=128), ys[:, :, :], [yst], [ysT_tiles[g][c]])

        for i in range(16 + 2):
            if i < 16:
                S1(i)
            if 0 <= i - 1 < 16:
                S2a(i - 1)
            if 0 <= i - 2 < 16:
                S3(i - 2)
            if 0 <= i - 1 < 16:
                S2b(i - 1)


def attn_phase(P, nc, es, ps, C, hT, hT_t, dr, y_attnT, yaT_tiles, sg_scr):
    pa, pa_t, q, q_t = ps
    pa0_t, pa1_t = pa_t
    w_in = dr["w_in"]
    ident, ident_t = C["ident"]
    SC = 1.0 / (128.0 ** 0.5)

    def cld(name, shape, dt, src, eng):
        t = sb(es, nc, "a_" + name, shape, dt); tl = T("a_" + name)
        P.dma(eng, t[tuple(slice(None) for _ in shape)], src, [], [tl])
        return t, tl
    identf, identf_t = cld("identf", [128, 128], F32, dr["c_ident"], "sp")
    RmT, RmT_t = cld("RmT", [128, 32], BF16, dr["c_RmT"], "pool")
    cosT, cosT_t = cld("cosT", [32, 2048], F32, dr["c_cos"], "sp")
    sinT, sinT_t = cld("sinT", [32, 2048], F32, dr["c_sin"], "sp")
    cmk, cmk_t = cld("cmk", [128, 2048], BF16, dr["c_cmask"], "pool")
    Asc, Asc_t = cld("Asc", [128, 16, 32], F32, dr["c_A"].rearrange("(t p) j -> p t j", p=128), "sp")
    Bsc, Bsc_t = cld("Bsc", [128, 16, 32], F32, dr["c_B"].rearrange("(t p) j -> p t j", p=128), "sp")
    Esel, Esel_t = cld("Esel", [128, 16, 128], BF16, dr["c_E"], "pool")
    dmask, dmask_t = cld("dmask", [128, 128], BF16, dr["c_diag"], "pool")
    lmask, lmask_t = cld("lmask", [128, 128], BF16, dr["c_low"], "pool")
    onesb, onesb_t = cld("onesb", [128, 128], BF16, dr["c_onesf"], "pool")
    mapa, mapa_t = cld("mapa", [128, 33], BF16, dr["c_map"], "pool")
    w1k, w1k_t = cld("w1k", [128, 32, 128], BF16, dr["cmp_wk1"].rearrange("(l d) m -> d l m", d=128), "pool")
    w1v, w1v_t = cld("w1v", [128, 32, 128], BF16, dr["cmp_wv1"].rearrange("(l d) m -> d l m", d=128), "pool")
    w2k, w2k_t = cld("w2k", [128, 128], BF16, dr["cmp_wk2"], "pool")
    w2v, w2v_t = cld("w2v", [128, 128], BF16, dr["cmp_wv2"], "pool")
    pek, pek_t = cld("pek", [128, 32], BF16, dr["cmp_pekT"], "pool")
    pev, pev_t = cld("pev", [128, 32], BF16, dr["cmp_pevT"], "pool")
    wch = [sb(es, nc, "awch%d" % i, [128, 16, 128], BF16) for i in range(2)]
    wch_t = [T("awch%d" % i) for i in range(2)]
    wi = [0]

    def load_w(col, n):
        wb, wbt = wch[wi[0] % 2], wch_t[wi[0] % 2]; wi[0] += 1
        P.dma("pool", wb[:, :, 0:n], w_in[:, col:col + n].rearrange("(kc p) n -> p kc n", p=128), [], [wbt])
        return wb, wbt

    sgT = sb(es, nc, "sgT", [48, 2048], BF16); sgT_t = T("sgT")
    wb, wbt = load_w(C_GN, 48)
    for t4 in range(4):
        pt, ptt = q[t4 % 2], q_t[t4 % 2]
        for kc in range(16):
            P.mm(pt[0:48, :], wb[:, kc, 0:48], hT[:, kc, t4 * 512:(t4 + 1) * 512], kc == 0, kc == 15,
                 [wbt] + hT_t[t4 * 4:t4 * 4 + 4], [ptt])
        P.act(sgT[:, t4 * 512:(t4 + 1) * 512], pt[0:48, :], AF.Sigmoid, [ptt], [sgT_t])
    sg_t = T("sg_scr")
    P.dma("sp", sg_scr, sgT[:, :], [sgT_t], [sg_t])

    cst = sb(es, nc, "cst", [128, 2], F32); cst_t = T("cst")
    for idx, (w1, w1_t, pe, pe_t) in enumerate(((w1k, w1k_t, pek, pek_t), (w1v, w1v_t, pev, pev_t))):
        for l in range(32):
            P.mm(q[2][:, 0:1], w1[:, l, :], pe[:, l:l + 1], l == 0, l == 31, [w1_t, pe_t], [q_t[2]])
        P.copy("dve", cst[:, idx:idx + 1], q[2][:, 0:1], [q_t[2]], [cst_t])

    qT = sb(es, nc, "qT", [128, 4, 2048], BF16); qT_t = [T("qT%d" % i) for i in range(4)]
    kT = {}
    for nm in ("kc", "ks", "kw", "vc"):
        kT[nm] = (sb(es, nc, nm + "T", [128, 2048], BF16), [T(nm + "T%d" % i) for i in range(4)])
    vtm = {}
    for nm in ("vs", "vw"):
        vtm[nm] = (sb(es, nc, nm + "tm", [128, 16, 128], BF16), [T(nm + "tm%d" % i) for i in range(16)])
    rtmp = [sb(es, nc, "rtmp%d" % i, [32, 512], F32) for i in range(2)]; rtmp_t = [T("rtmp%d" % i) for i in range(2)]
    hid = [sb(es, nc, "hid%d" % i, [128, 128], BF16) for i in range(2)]; hid_t = [T("hid%d" % i) for i in range(2)]
    kcmpT = sb(es, nc, "kcmpT", [128, 128], BF16); kcmpT_t = T("kcmpT")
    vcmp = sb(es, nc, "vcmp", [128, 128], BF16); vcmp_t = T("vcmp")
    Pc = sb(es, nc, "Pc", [128, 4, 128], BF16); Pc_t = T("Pc")
    PTb = [sb(es, nc, "PT%d" % i, [128, 4, 128], BF16) for i in range(3)]; PTb_t = [T("PT%d" % i) for i in range(3)]
    gb = [sb(es, nc, "gb%d" % i, [128, 3, 4, 128], BF16) for i in range(2)]; gb_t = [T("gb%d" % i) for i in range(2)]
    smi = sb(es, nc, "smi", [128, 8], F32); smi_t = T("smi")
    imp = sb(es, nc, "imp", [128, 4, 32], F32); imp_t = T("imp")
    m8 = sb(es, nc, "m8", [128, 2, 8], F32); m8_t = T("m8")
    mbf = sb(es, nc, "mbf", [128, 32], F32); mbf_t = T("mbf")
    mbT = sb(es, nc, "mbT", [128, 4, 128], BF16); mbT_t = T("mbT")
    P.memset("pool", mbT[:, :, :], 0.0, [mbT_t])
    lnd = [sb(es, nc, "lnd%d" % i, [128, 512], F32) for i in range(3)]; lnd_t = [T("lnd%d" % i) for i in range(3)]
    osb = [sb(es, nc, "osb%d" % i, [128, 512], BF16) for i in range(3)]; osb_t = [T("osb%d" % i) for i in range(3)]
    rdb = [sb(es, nc, "rdb%d" % i, [128, 512], BF16) for i in range(3)]; rdb_t = [T("rdb%d" % i) for i in range(3)]
    f2 = sb(es, nc, "f2", [128, 512], F32); f2_t = T("f2")
    facc = sb(es, nc, "facc", [128, 512], F32); facc_t = T("facc")
    yaT = [sb(es, nc, "yaT%d" % i, [128, 4, 128], BF16) for i in range(2)]; yaT_t = [T("yaT%d" % i) for i in range(2)]
    P.memset("pool", hid[0][:, :], 0.0, [hid_t[0]])
    P.memset("pool", hid[1][:, :], 0.0, [hid_t[1]])
    pti = [0]

    for g in range(4):
        plist = [("q", r, C_Q + (4 * g + r) * 128, True, SC) for r in range(4)]
        plist += [("kc", 0, C_KC + g * 128, True, 1.0), ("ks", 0, C_KS + g * 128, True, 1.0),
                  ("kw", 0, C_KW + g * 128, True, 1.0), ("vc", 0, C_VC + g * 128, False, 1.0)]
        for (nm, r, col, rope, scl) in plist:
            wb, wbt = load_w(col, 128)
            for t4 in range(4):
                tsl = slice(t4 * 512, (t4 + 1) * 512)
                pt, ptt = q[t4 % 2], q_t[t4 % 2]
                if nm == "q":
                    dst, dst_t = qT[:, r, tsl], qT_t[t4]
                    dst32 = qT[0:32, r, tsl]
                else:
                    dst, dst_t = kT[nm][0][:, tsl], kT[nm][1][t4]
                    dst32 = kT[nm][0][0:32, tsl]
                for kc in range(16):
                    P.mm(pt[:, :], wb[:, kc, :], hT[:, kc, tsl], kc == 0, kc == 15,
                         [wbt] + hT_t[t4 * 4:t4 * 4 + 4], [ptt])
                P.act(dst, pt[:, :], AF.Copy, [ptt], [dst_t], scale=scl)
                if rope:
                    rp, rpt = q[2 + t4 % 2], q_t[2 + t4 % 2]
                    rt, rtt = rtmp[t4 % 2], rtmp_t[t4 % 2]
                    P.mm(rp[0:32, :], RmT[:, :], dst, True, True, [RmT_t, dst_t], [rpt])
                    P.tt("dve", rt[:, :], rp[0:32, :], sinT[:, tsl], ALU.mult, [rpt, sinT_t], [rtt])
                    P.tt("dve", rp[0:32, :], dst32, cosT[:, tsl], ALU.mult, [dst_t, cosT_t, rpt], [rpt])
                    P.tt("dve", dst32, rt[:, :], rp[0:32, :], ALU.add, [rtt, rpt], [dst_t])
        for nm, col in (("vs", C_VS + g * 128), ("vw", C_VW + g * 128)):
            wb, wbt = load_w(col, 128)
            for tt in range(16):
                pt, ptt = q[tt % 2], q_t[tt % 2]
                for kc in range(16):
                    P.mm(pt[:, 0:128], hT[:, kc, tt * 128:(tt + 1) * 128], wb[:, kc, :], kc == 0, kc == 15,
                         [wbt, hT_t[tt]], [ptt])
                P.copy("act", vtm[nm][0][:, tt, :], pt[:, 0:128], [ptt], [vtm[nm][1][tt]])
        for idx, (w1, w1_t, w2, w2_t, src) in enumerate(((w1k, w1k_t, w2k, w2k_t, "kc"), (w1v, w1v_t, w2v, w2v_t, "vc"))):
            sT, sT_t = kT[src]
            hp, hpt = q[2], q_t[2]
            for l in range(32):
                rhs = bass.AP(sT, l, [[sT.ap().ap[0][0], 128], [16, 127]])
                P.mm(hp[:, 0:127], w1[:, l, :], rhs, l == 0, l == 31, [w1_t] + sT_t, [hpt])
            P.act(hid[idx][:, 0:127], hp[:, 0:127], AF.Silu, [hpt, cst_t], [hid_t[idx]], bias=cst[:, idx:idx + 1])
            op_, opt_ = q[3], q_t[3]
            if idx == 0:
                P.mm(op_[:, 0:128], w2[:, :], hid[0][:, :], True, True, [w2_t, hid_t[0]], [opt_])
                P.copy("act", kcmpT[:, :], op_[:, 0:128], [opt_], [kcmpT_t])
            else:
                P.mm(op_[:, 0:128], hid[1][:, :], w2[:, :], True, True, [w2_t, hid_t[1]], [opt_])
                P.copy("act", vcmp[:, :], op_[:, 0:128], [opt_], [vcmp_t])
        for qt in range(16):
            qs = slice(qt * 128, (qt + 1) * 128)
            rq = qT[:, :, qs]
            rq_t = [qT_t[qt // 4]]
            gbb, gbt = gb[qt % 2], gb_t[qt % 2]
            for b3 in range(3):
                gsrc = bass.AP(sg_scr.tensor, sg_scr.offset + (16 * b3 + 4 * g) * 2048 + qt * 128,
                               [[0, 128], [2048, 4], [1, 128]])
                P.dma("sp", gbb[:, b3, :, :], gsrc, [sg_t] + ([gbt] if b3 else []), [gbt])
            s_, st_ = q[pti[0] % 2], q_t[pti[0] % 2]; pti[0] += 1
            P.mm(s_[:, :], kcmpT[:, :], rq, True, False, [kcmpT_t] + rq_t, [st_])
            P.mm(s_[:, :], ident[:, :], bc_mid(cmk[:, qs], 4), False, True, [ident_t, cmk_t], [st_])
            P.act(Pc[:, :, :], s_[:, :].rearrange("p (r t) -> p r t", r=4), AF.Exp, [st_], [Pc_t])
            P.mm(pa[:, 0:512], vcmp[:, :], Pc[:, :, :], True, True, [vcmp_t, Pc_t], [pa0_t])
            P.mm(pa[:, 512:1024], onesb[:, :], Pc[:, :, :], True, True, [onesb_t, Pc_t], [pa1_t])
            for r in range(4):
                P.mm(q[2][:, r * 33:(r + 1) * 33], Pc[:, r, :], mapa[:, :], True, True, [Pc_t, mapa_t], [q_t[2]])
            ipv = q[2][:, 0:132].rearrange("p (r j) -> p r j", r=4)
            P.ts("dve", smi[:, 0:4], ipv[:, :, 32], 1e-20, None, ALU.max, None, [q_t[2]], [smi_t])
            P.recip(smi[:, 4:8], smi[:, 0:4], [smi_t], [smi_t])
            P.ts("dve", imp[:, 0, :], ipv[:, 0, 0:32], smi[:, 4:5], None, ALU.mult, None, [q_t[2], smi_t], [imp_t])
            for r in range(1, 4):
                P.stt("dve", imp[:, 0, :], ipv[:, r, 0:32], smi[:, 4 + r:5 + r], imp[:, 0, :], ALU.mult, ALU.add,
                      [q_t[2], smi_t, imp_t], [imp_t])
            P.tt("dve", imp[:, 0, :], imp[:, 0, :], Asc[:, qt, :], ALU.mult, [imp_t, Asc_t], [imp_t])
            P.tt("dve", imp[:, 0, :], imp[:, 0, :], Bsc[:, qt, :], ALU.add, [imp_t, Bsc_t], [imp_t])
            P.op("dve", lambda e, o=m8[:, 0, :], i=imp[:, 0, :]: e.max(o, i), [imp_t], [m8_t])
            P.op("dve", lambda e, o=imp[:, 1, :], a=m8[:, 0, :], b=imp[:, 0, :]: e.match_replace(o, a, b, -3.0e38),
                 [imp_t, m8_t], [imp_t])
            P.op("dve", lambda e, o=m8[:, 1, :], i=imp[:, 1, :]: e.max(o, i), [imp_t], [m8_t])
            P.ts("dve", imp[:, 2, :], imp[:, 0, :], m8[:, 1, 7:8], None, ALU.is_ge, None, [imp_t, m8_t], [imp_t])
            P.ts("dve", mbf[:, :], imp[:, 2, :], -1.0, 30000.0, ALU.add, ALU.mult, [imp_t], [mbf_t])
            k0 = max(0, qt - 4)
            tiles_ = [("w", kt) for kt in range(k0, qt + 1)] + [("s", kt) for kt in range(qt + 1)]
            pend = None
            for tl_ in tiles_ + [(None, None)]:
                br, kt = tl_
                cur = None
                if br is not None:
                    ks_ = slice(kt * 128, (kt + 1) * 128)
                    s_, st_ = q[pti[0] % 2], q_t[pti[0] % 2]
                    pb, pbt = PTb[pti[0] % 3], PTb_t[pti[0] % 3]; pti[0] += 1
                    mk_ = None
                    if kt == qt:
                        mk_ = (dmask, dmask_t)
                    elif br == "w" and kt == qt - 4:
                        mk_ = (lmask, lmask_t)
                    if br == "w":
                        P.mm(s_[:, :], kT["kw"][0][:, ks_], rq, True, mk_ is None, [kT["kw"][1][kt // 4]] + rq_t, [st_])
                    else:
                        if kt == 0:
                            P.tr(q[3][0:32, 0:128], mbf[:, :], identf[:, :], [mbf_t, identf_t], [q_t[3]])
                            P.copy("act", mbT[0:32, :, :], bc_mid(q[3][0:32, 0:128], 4), [q_t[3]], [mbT_t])
                        P.mm(s_[:, :], kT["ks"][0][:, ks_], rq, True, False, [kT["ks"][1][kt // 4]] + rq_t, [st_])
                        P.mm(s_[:, :], Esel[:, kt, :], mbT[:, :, :], False, mk_ is None, [Esel_t, mbT_t], [st_])
                    if mk_ is not None:
                        P.mm(s_[:, :], ident[:, :], bc_mid(mk_[0][:, :], 4), False, True, [ident_t, mk_[1]], [st_])
                    P.act(pb[:, :, :], s_[:, :].rearrange("p (r t) -> p r t", r=4), AF.Exp, [st_], [pbt])
                    cur = (br, kt, pb, pbt)
                if pend is not None:
                    pbr, pkt, ppb, ppbt = pend
                    if pbr == "w":
                        P.mm(q[4][:, :], vtm["vw"][0][:, pkt, :], ppb[:, :, :], pkt == k0, pkt == qt, [vtm["vw"][1][pkt], ppbt], [q_t[4]])
                        P.mm(q[5][:, :], onesb[:, :], ppb[:, :, :], pkt == k0, pkt == qt, [onesb_t, ppbt], [q_t[5]])
                    else:
                        P.mm(q[2][:, :], vtm["vs"][0][:, pkt, :], ppb[:, :, :], pkt == 0, pkt == qt, [vtm["vs"][1][pkt], ppbt], [q_t[2]])
                        P.mm(q[3][:, :], onesb[:, :], ppb[:, :, :], pkt == 0, pkt == qt, [onesb_t, ppbt], [q_t[3]])
                pend = cur
            branches = ((0, pa[:, 0:512], pa0_t, pa[:, 512:1024], pa1_t), (1, q[2][:, :], q_t[2], q[3][:, :], q_t[3]),
                        (2, q[4][:, :], q_t[4], q[5][:, :], q_t[5]))
            for (b, O_, O_t, Dn, Dn_t) in branches:
                P.act(lnd[b][:, :], Dn, AF.Ln, [Dn_t], [lnd_t[b]], bias=1e-18)
                P.act(osb[b][:, :], O_, AF.Copy, [O_t], [osb_t[b]])
            for (b, O_, O_t, Dn, Dn_t) in branches:
                P.act(rdb[b][:, :], lnd[b][:, :], AF.Exp, [lnd_t[b]], [rdb_t[b]], scale=-1.0)
            ya, yat = yaT[qt % 2], yaT_t[qt % 2]
            for (b, O_, O_t, Dn, Dn_t) in branches:
                P.tt("dve", rdb[b][:, :], rdb[b][:, :], gbb[:, b, :, :].rearrange("p r t -> p (r t)"), ALU.mult, [rdb_t[b], gbt], [rdb_t[b]])
                if b == 0:
                    P.tt("dve", facc[:, :], osb[b][:, :], rdb[b][:, :], ALU.mult, [osb_t[b], rdb_t[b]], [facc_t])
                else:
                    P.tt("dve", f2[:, :], osb[b][:, :], rdb[b][:, :], ALU.mult, [osb_t[b], rdb_t[b]], [f2_t])
                    if b == 1:
                        P.tt("dve", facc[:, :], facc[:, :], f2[:, :], ALU.add, [facc_t, f2_t], [facc_t])
                    else:
                        P.tt("dve", ya[:, :, :].rearrange("p r t -> p (r t)"), facc[:, :], f2[:, :], ALU.add, [facc_t, f2_t], [yat])
                        P.dma("sp", y_attnT[g * 512:(g + 1) * 512, qs].rearrange("(r p) t -> p r t", p=128), ya[:, :, :],
                              [yat], [yaT_tiles[g][qt]])


def merge_gates(P, nc, es, ps, hT, hT_t, dr, gs_scr, ga_scr, g_tiles):
    pa, pa_t, q, q_t = ps
    w_in = dr["w_in"]
    wgs = [sb(es, nc, "mwgs%d" % i, [128, 16, 128], BF16) for i in range(2)]; wgs_t = [T("mwgs%d" % i) for i in range(2)]
    wga = [sb(es, nc, "mwga%d" % i, [128, 16, 128], BF16) for i in range(2)]; wga_t = [T("mwga%d" % i) for i in range(2)]
    gsb = [sb(es, nc, "gsb%d" % i, [128, 2048], BF16) for i in range(2)]; gsb_t = [T("gsb%d" % i) for i in range(2)]
    gab = [sb(es, nc, "gab%d" % i, [128, 2048], BF16) for i in range(2)]; gab_t = [T("gab%d" % i) for i in range(2)]
    for fc in range(16):
        i = fc % 2
        P.dma("pool", wgs[i][:, :, :], w_in[:, C_GS + fc * 128:C_GS + (fc + 1) * 128].rearrange("(kc p) n -> p kc n", p=128), [], [wgs_t[i]])
        P.dma("pool", wga[i][:, :, :], w_in[:, C_GA + fc * 128:C_GA + (fc + 1) * 128].rearrange("(kc p) n -> p kc n", p=128), [], [wga_t[i]])
        for t4 in range(4):
            tsl = slice(t4 * 512, (t4 + 1) * 512)
            j = t4 % 2
            for kc in range(16):
                P.mm(q[j][:, :], wgs[i][:, kc, :], hT[:, kc, tsl], kc == 0, kc == 15, [wgs_t[i]] + hT_t[t4 * 4:t4 * 4 + 4], [q_t[j]])
            for kc in range(16):
                P.mm(q[2 + j][:, :], wga[i][:, kc, :], hT[:, kc, tsl], kc == 0, kc == 15, [wga_t[i]] + hT_t[t4 * 4:t4 * 4 + 4], [q_t[2 + j]])
            P.act(gsb[i][:, tsl], q[j][:, :], AF.Sigmoid, [q_t[j]], [gsb_t[i]])
            P.act(gab[i][:, tsl], q[2 + j][:, :], AF.Sigmoid, [q_t[2 + j]], [gab_t[i]])
        P.dma("sp", gs_scr[fc * 128:(fc + 1) * 128, :], gsb[i][:, :], [gsb_t[i]], [g_tiles[0][fc]])
        P.dma("sp", ga_scr[fc * 128:(fc + 1) * 128, :], gab[i][:, :], [gab_t[i]], [g_tiles[1][fc]])


def merge_phase(P, nc, es, ps, A, A_t, dr, y_ssmT, ys_tiles, y_attnT, ya_tiles, m_scr, m_tiles, gs_scr, ga_scr, g_tiles):
    pa, pa_t, q, q_t = ps
    yab = sb(es, nc, "yab", [128, 16, 1024], BF16); yab_t = T("yab")
    ws = [sb(es, nc, "mws%d" % i, [128, 32, 128], BF16) for i in range(2)]; ws_t = [T("mws%d" % i) for i in range(2)]
    wa = [sb(es, nc, "mwa%d" % i, [128, 16, 128], BF16) for i in range(2)]; wa_t = [T("mwa%d" % i) for i in range(2)]
    gst = [sb(es, nc, "gst%d" % i, [128, 1024], BF16) for i in range(2)]; gst_t = [T("gst%d" % i) for i in range(2)]
    gat = [sb(es, nc, "gat%d" % i, [128, 1024], BF16) for i in range(2)]; gat_t = [T("gat%d" % i) for i in range(2)]
    m1 = sb(es, nc, "m1", [128, 512], F32); m1_t = T("m1")
    m2 = sb(es, nc, "m2", [128, 512], F32); m2_t = T("m2")
    mo = [sb(es, nc, "mo%d" % i, [128, 512], BF16) for i in range(2)]; mo_t = [T("mo%d" % i) for i in range(2)]
    pstep = A.ap().ap[0][0]
    ysb = bass.AP(A, 0, [[pstep, 128], [1024, 32], [1, 1024]])

    def ysb_k(kc, a, b):
        return bass.AP(A, kc * 1024 + a, [[pstep, 128], [1, b - a]])
    it = 0
    for th in range(2):
        hsl = slice(th * 1024, (th + 1) * 1024)
        for half in range(2):
            P.dma("sp", bass.AP(A, half * 16 * 1024, [[pstep, 128], [1024, 16], [1, 1024]]),
                  y_ssmT[half * 2048:(half + 1) * 2048, hsl].rearrange("(kc p) t -> p kc t", p=128),
                  [ys_tiles[g][c] for g in range(half * 4, half * 4 + 4) for c in range(th * 8, th * 8 + 8)], A_t)
        P.dma("sp", yab[:, :, :], y_attnT[:, hsl].rearrange("(kc p) t -> p kc t", p=128),
              [ya_tiles[g][c] for g in range(4) for c in range(th * 8, th * 8 + 8)], [yab_t])
        for fc in range(16):
            i = fc % 2
            csl = slice(fc * 128, (fc + 1) * 128)
            P.dma("pool", ws[i][:, :, :], dr["w_ssm_branch"][:, csl].rearrange("(kc p) n -> p kc n", p=128), [], [ws_t[i]])
            P.dma("pool", wa[i][:, :, :], dr["w_attn_branch"][:, csl].rearrange("(kc p) n -> p kc n", p=128), [], [wa_t[i]])
            P.dma("sp", gst[i][:, :], gs_scr[csl, hsl], [g_tiles[0][fc]], [gst_t[i]])
            P.dma("sp", gat[i][:, :], ga_scr[csl, hsl], [g_tiles[1][fc]], [gat_t[i]])
            for t2 in range(2):
                j = it % 2; it += 1
                a, b = t2 * 512, (t2 + 1) * 512
                for kc in range(32):
                    P.mm(q[j][:, :], ws[i][:, kc, :], ysb_k(kc, a, b), kc == 0, kc == 31, [ws_t[i]] + A_t, [q_t[j]])
                for kc in range(16):
                    P.mm(q[2 + j][:, :], wa[i][:, kc, :], yab[:, kc, a:b], kc == 0, kc == 15, [wa_t[i], yab_t], [q_t[2 + j]])
                P.tt("dve", m1[:, :], q[j][:, :], gst[i][:, a:b], ALU.mult, [q_t[j], gst_t[i]], [m1_t])
                P.tt("dve", m2[:, :], q[2 + j][:, :], gat[i][:, a:b], ALU.mult, [q_t[2 + j], gat_t[i]], [m2_t])
                P.tt("dve", mo[j][:, :], m1[:, :], m2[:, :], ALU.add, [m1_t, m2_t], [mo_t[j]])
                t4 = th * 2 + t2
                P.dma("sp", m_scr[csl, t4 * 512:(t4 + 1) * 512], mo[j][:, :], [mo_t[j]], [m_tiles[fc][t4]])


def linear_tm(P, nc, es, ps, pfx, A, A_t, W, KC, NW, nft, epilogue, extra=None):
    pa, pa_t, q, q_t = ps
    wb = [sb(es, nc, pfx + "w%d" % i, [128, KC, NW], BF16) for i in range(2)]
    wb_t = [T(pfx + "w%d" % i) for i in range(2)]
    it = 0
    for ft in range(nft):
        i = ft % 2
        nsp = max(1, (KC * NW) // (16 * 512))
        kstep = KC // nsp
        for sp_ in range(nsp):
            ksl = slice(sp_ * kstep, (sp_ + 1) * kstep)
            P.dma("pool", wb[i][:, ksl, :], W[sp_ * kstep * 128:(sp_ + 1) * kstep * 128, ft * NW:(ft + 1) * NW].rearrange("(kc p) n -> p kc n", p=128),
                  [wb_t[i]] if sp_ else [], [wb_t[i]])
        nxt = A(0)
        if extra is not None:
            extra(0, ft, it + 1)
        for tt in range(16):
            pt, ptt = q[it % 2], q_t[it % 2]; it += 1
            afn, atiles = nxt
            if tt + 1 < 16:
                nxt = A(tt + 1)
                if extra is not None:
                    extra(tt + 1, ft, it + 1)
            for kc in range(KC):
                P.mm(pt[:, 0:NW], afn(kc), wb[i][:, kc, :], kc == 0, kc == KC - 1, [wb_t[i]] + atiles, [ptt])
            epilogue(tt, ft, pt[:, 0:NW], ptt, it)


def final_norm(P, nc, es, src_rows, wrow, wrow_t, out_dram, out_tiles, pfx):
    xb = [sb(es, nc, pfx + "xb%d" % i, [128, 2048], F32) for i in range(2)]
    xb_t = [T(pfx + "xb%d" % i) for i in range(2)]
    junk = sb(es, nc, pfx + "junk", [128, 2048], BF16); junk_t = T(pfx + "junk")
    ob = [sb(es, nc, pfx + "ob%d" % i, [128, 2048], F32) for i in range(2)]
    ob_t = [T(pfx + "ob%d" % i) for i in range(2)]
    st = sb(es, nc, pfx + "st", [128, 16, 4], F32)
    st_t = [T(pfx + "st%d" % i) for i in range(16)]
    for tt in range(16):
        i = tt % 2
        ap, rt = src_rows(tt)
        P.dma("sp", xb[i][:, :], ap, rt, [xb_t[i]])
        P.act(junk[:, :], xb[i][:, :], AF.Square, [xb_t[i]], [junk_t, st_t[tt]], accum_out=st[:, tt, 0:1])
        P.act(st[:, tt, 1:2], st[:, tt, 0:1], AF.Sqrt, [st_t[tt]], [st_t[tt]], bias=EPS, scale=1.0 / D)
        P.recip(st[:, tt, 2:3], st[:, tt, 1:2], [st_t[tt]], [st_t[tt]])
        P.stt("dve", ob[i][:, :], xb[i][:, :], st[:, tt, 2:3], wrow[:, :], ALU.mult, ALU.mult,
              [xb_t[i], st_t[tt], wrow_t], [ob_t[i]])
        P.dma("sp", out_dram[tt * 128:(tt + 1) * 128, :], ob[i][:, :], [ob_t[i]], [out_tiles[tt]])


def ffn_up_phase(P, nc, es, ps, C, h2T, h2T_t, dr, act_scr, act_tiles):
    pa, pa_t, q, q_t = ps
    fcw = sb(es, nc, "fcw", [128, 44, 3], F32); fcw_t = T("fcw")
    fcb = sb(es, nc, "fcb", [128, 44], F32); fcb_t = T("fcb")
    P.dma("sp", fcw[:, :, :], dr["ffn_cw"].rearrange("p (j k) -> p j k", k=3), [], [fcw_t])
    P.dma("sp", fcb[:, :], dr["ffn_cb"], [], [fcb_t])
    wg = [sb(es, nc, "fwg%d" % i, [128, 16, 128], BF16) for i in range(2)]; wg_t = [T("fwg%d" % i) for i in range(2)]
    wu = [sb(es, nc, "fwu%d" % i, [128, 16, 128], BF16) for i in range(2)]; wu_t = [T("fwu%d" % i) for i in range(2)]
    raw = [sb(es, nc, "fraw%d" % i, [128, 514], F32) for i in range(2)]; raw_t = [T("fraw%d" % i) for i in range(2)]
    acc = [sb(es, nc, "facc%d" % i, [128, 512], F32) for i in range(2)]; acc_t = [T("facc%d" % i) for i in range(2)]
    sg = [sb(es, nc, "fsg%d" % i, [128, 512], F32) for i in range(2)]; sg_t = [T("fsg%d" % i) for i in range(2)]
    ao = [sb(es, nc, "fao%d" % i, [128, 16, 128], BF16) for i in range(2)]; ao_t = [T("fao%d" % i) for i in range(2)]
    for fc in range(44):
        i = fc % 2
        csl = slice(fc * 128, (fc + 1) * 128)
        P.dma("pool", wg[i][:, :, :], dr["ffn_w_gate"][:, csl].rearrange("(kc p) n -> p kc n", p=128), [], [wg_t[i]])
        P.dma("pool", wu[i][:, :, :], dr["ffn_w_up"][:, csl].rearrange("(kc p) n -> p kc n", p=128), [], [wu_t[i]])
        P.memset("pool", raw[0][:, 0:2], 0.0, [raw_t[0]])
        for t4 in range(4):
            tsl = slice(t4 * 512, (t4 + 1) * 512)
            j = t4 % 2
            for kc in range(16):
                P.mm(q[j][:, :], wg[i][:, kc, :], h2T[:, kc, tsl], kc == 0, kc == 15, [wg_t[i]] + h2T_t[t4 * 4:t4 * 4 + 4], [q_t[j]])
            for kc in range(16):
                P.mm(q[2 + j][:, :], wu[i][:, kc, :], h2T[:, kc, tsl], kc == 0, kc == 15, [wu_t[i]] + h2T_t[t4 * 4:t4 * 4 + 4], [q_t[2 + j]])
            rw, rwt = raw[j], raw_t[j]
            P.copy("act", rw[:, 2:514], q[j][:, :], [q_t[j]], [rwt])
            if t4 < 3:
                P.copy("pool", raw[1 - j][:, 0:2], rw[:, 512:514], [rwt], [raw_t[1 - j]])
            P.ts("dve", acc[j][:, :], rw[:, 2:514], fcw[:, fc, 2:3], fcb[:, fc:fc + 1], ALU.mult, ALU.add, [rwt, fcw_t, fcb_t], [acc_t[j]])
            for k in (1, 0):
                P.stt("dve", acc[j][:, :], rw[:, k:k + 512], fcw[:, fc, k:k + 1], acc[j][:, :], ALU.mult, ALU.add,
                      [rwt, fcw_t, acc_t[j]], [acc_t[j]])
            P.act(sg[j][:, :], acc[j][:, :], AF.Silu, [acc_t[j]], [sg_t[j]])
            P.tt("dve", ao[i][:, t4 * 4:(t4 + 1) * 4, :].rearrange("p a b -> p (a b)"), q[2 + j][:, :], sg[j][:, :], ALU.mult,
                 [q_t[2 + j], sg_t[j]], [ao_t[i]])
        P.dma("sp", act_scr[:, :, fc, :].rearrange("t p c -> p t c"), ao[i][:, :, :], [ao_t[i]], [act_tiles[fc]])

import numpy as np

def consts_np():
    s = np.arange(128)
    c = {}
    c["c_ident"] = np.eye(128, dtype=np.float32)
    c["c_U"] = (s[:, None] <= s[None, :]).astype(np.float32)
    c["c_onesf"] = np.ones((128, 128), np.float32)
    c["c_maskneg"] = np.where(s[None, :] >= s[:, None], 0.0, -30000.0).astype(np.float32)
    R = np.zeros((128, 32), np.float32)
    for m in range(16):
        R[m + 16, m] = -1.0
        R[m, m + 16] = 1.0
    c["c_RmT"] = R
    half = 16
    inv = 500000.0 ** (-(np.arange(half, dtype=np.float32) * 2.0 / 32.0))
    ang = np.arange(2048, dtype=np.float32)[None, :] * inv[:, None].astype(np.float32)
    c["c_cos"] = np.concatenate([np.cos(ang), np.cos(ang)], 0).astype(np.float32)
    c["c_sin"] = np.concatenate([np.sin(ang), np.sin(ang)], 0).astype(np.float32)
    i = np.arange(128)[:, None]; t = np.arange(2048)[None, :]
    c["c_cmask"] = np.where((16 * i + 31 <= t) & (i < 127), 0.0, -30000.0).astype(np.float32)
    tt = np.arange(2048)[:, None]; j = np.arange(32)[None, :]
    cur = tt // 64
    causal = j <= cur
    forced = (j == 0) | (j == cur) | (j == cur - 1)
    c["c_A"] = (causal & ~forced).astype(np.float32)
    c["c_B"] = (forced * 1.0e4 + (~causal) * (-1.0e30)).astype(np.float32)
    E = np.zeros((128, 16, 128), np.float32)
    for kt in range(16):
        for p in range(128):
            E[2 * kt + p // 64, kt, p] = 1.0
    c["c_E"] = E
    c["c_diag"] = np.where(s[:, None] <= s[None, :], 0.0, -30000.0).astype(np.float32)
    c["c_low"] = np.where(s[:, None] > s[None, :], 0.0, -30000.0).astype(np.float32)
    m = np.zeros((128, 33), np.float32)
    ii = np.arange(128)[:, None]; jj = np.arange(32)[None, :]
    m[:, :32] = ((16 * ii < 64 * jj + 64) & (16 * ii + 32 > 64 * jj) & (ii < 127)).astype(np.float32)
    m[:, 32] = 1.0
    c["c_map"] = m
    return c


def build_program():
    nc = bass.Bass("TRN2", target_bir_lowering=False)
    dr = {}

    def din(name, shape):
        dr[name] = nc.dram_tensor(name, list(shape), F32, kind="ExternalInput").ap()

    def dscr(name, shape, dt):
        return nc.dram_tensor(name, list(shape), dt, kind="Internal").ap()

    din("x", [2048, 2048]); din("p", [2048, 256]); din("w_in", [2048, NIN])
    for nm in ("norm_mix_w", "norm_ffn_w", "ple_norm_w", "final_norm_w"):
        din(nm, [1, 2048])
    din("ssm_cw", [128, 192]); din("ssm_cb", [128, 48]); din("ssm_dt_bias", [1, 64]); din("ssm_a_log", [1, 64])
    din("ssm_d", [1, 64]); din("ssm_norm_w", [1, 4096])
    din("cmp_wk1", [4096, 128]); din("cmp_wv1", [4096, 128]); din("cmp_wk2", [128, 128]); din("cmp_wv2", [128, 128])
    din("cmp_pekT", [128, 32]); din("cmp_pevT", [128, 32])
    din("w_ssm_branch", [4096, 2048]); din("w_attn_branch", [2048, 2048]); din("w_mix_out", [2048, 2048])
    din("ffn_w_gate", [2048, DFF]); din("ffn_w_up", [2048, DFF]); din("ffn_w_down", [DFF, 2048])
    din("ffn_cw", [128, 132]); din("ffn_cb", [128, 44])
    din("ple_w_gate", [2048, 2048]); din("ple_w_proj", [256, 2048])
    for k, v in consts_np().items():
        din(k, v.shape)
    out = nc.dram_tensor("out", [2048, 2048], F32, kind="ExternalOutput").ap()
    y_ssmT = dscr("y_ssmT", [4096, 2048], BF16)
    y_attnT = dscr("y_attnT", [2048, 2048], BF16)
    sg_scr = dscr("sg_scr", [48, 2048], BF16)
    ac_scr = dscr("ac_scr", [16, 8, 8, 128], F32)
    m_scr = dscr("m_scr", [2048, 2048], BF16)
    gs_scr = dscr("gs_scr", [2048, 2048], BF16)
    ga_scr = dscr("ga_scr", [2048, 2048], BF16)
    x1_scr = dscr("x1_scr", [2048, 2048], F32)
    x2_scr = dscr("x2_scr", [2048, 2048], F32)
    x3_scr = dscr("x3_scr", [2048, 2048], F32)
    act_scr = dscr("act_scr", [16, 128, 44, 128], BF16)

    P = B(nc)
    with ExitStack() as es:
        pa = es.enter_context(nc.psum_tensor("pa", [128, 1024], F32))
        q = [es.enter_context(nc.psum_tensor("q%d" % i, [128, 512], F32)) for i in range(6)]
        q_t = [T("q%d" % i) for i in range(6)]
        ps = (pa, (T("pa0"), T("pa1")), q, q_t)
        ps01 = [(q[0], q_t[0]), (q[1], q_t[1])]
        C = load_consts(P, nc, es, dr)
        ident, ident_t = C["ident"]
        wrow = sb(es, nc, "wrow", [128, 2048], F32); wrow_t = T("wrow")
        A = sb(es, nc, "Abuf", [128, 16, 2048], BF16); A_t = [T("A%d" % i) for i in range(16)]

        P.dma("sp", wrow[:, :], dr["norm_mix_w"].broadcast_to([128, 2048]), [], [wrow_t])
        with ExitStack() as es2:
            norm_to_T(P, nc, es2, ps01, ident, lambda tt: (dr["x"][tt * 128:(tt + 1) * 128, :], []), wrow, wrow_t, A, A_t, "n0", ident_t)
        P.barrier()
        ys_tiles = [[T("ys%d_%d" % (g, c)) for c in range(16)] for g in range(8)]
        with ExitStack() as es2:
            ssm_phase(P, nc, es2, ps, C, A, A_t, dr, y_ssmT, ys_tiles, ac_scr)
        P.barrier()
        ya_tiles = [[T("ya%d_%d" % (g, c)) for c in range(16)] for g in range(4)]
        with ExitStack() as es2:
            attn_phase(P, nc, es2, ps, C, A, A_t, dr, y_attnT, ya_tiles, sg_scr)
        P.barrier()
        m_tiles = [[T("m%d_%d" % (fc, t4)) for t4 in range(4)] for fc in range(16)]
        g_tiles = [[T("gs%d" % fc) for fc in range(16)], [T("ga%d" % fc) for fc in range(16)]]
        with ExitStack() as es2:
            merge_gates(P, nc, es2, ps, A, A_t, dr, gs_scr, ga_scr, g_tiles)
        P.barrier()
        with ExitStack() as es2:
            merge_phase(P, nc, es2, ps, A, A_t, dr, y_ssmT, ys_tiles, y_attnT, ya_tiles, m_scr, m_tiles, gs_scr, ga_scr, g_tiles)
        P.barrier()
        x1_tiles = [[T("x1_%d_%d" % (tt, ft)) for ft in range(4)] for tt in range(16)]
        for t4 in range(4):
            tsl = slice(t4 * 512, (t4 + 1) * 512)
            P.dma("sp", A[:, :, tsl], m_scr[:, tsl].rearrange("(kc p) t -> p kc t", p=128),
                  [m_tiles[fc][t4] for fc in range(16)], A_t[t4 * 4:t4 * 4 + 4])
        esG1 = ExitStack()
        for es2 in [esG1]:
            xin = [sb(es2, nc, "p4xin%d" % i, [128, 512], F32) for i in range(3)]; xin_t = [T("xin%d" % i) for i in range(3)]
            xo = [sb(es2, nc, "p4xo%d" % i, [128, 512], F32) for i in range(3)]; xo_t = [T("xo%d" % i) for i in range(3)]

            def pre4(tt, ft, it):
                i = it % 3
                P.dma("sp", xin[i][:, :], dr["x"][tt * 128:(tt + 1) * 128, ft * 512:(ft + 1) * 512], [], [xin_t[i]])

            def epi4(tt, ft, pt, ptt, it):
                i = it % 3
                P.tt("dve", xo[i][:, :], pt, xin[i][:, :], ALU.add, [ptt, xin_t[i]], [xo_t[i]])
                P.dma("sp", x1_scr[tt * 128:(tt + 1) * 128, ft * 512:(ft + 1) * 512], xo[i][:, :], [xo_t[i]], [x1_tiles[tt][ft]])
            linear_tm(P, nc, es2, ps, "l4", lambda tt: ((lambda kc: A[:, kc, tt * 128:(tt + 1) * 128]), [A_t[tt]]), None,
                      dr["w_mix_out"], 16, 512, 4, epi4, extra=pre4)
        P.dma("sp", wrow[:, :], dr["norm_ffn_w"].broadcast_to([128, 2048]), [], [wrow_t])
        for es2 in [esG1]:
            norm_to_T(P, nc, es2, ps01, ident, lambda tt: (x1_scr[tt * 128:(tt + 1) * 128, :], x1_tiles[tt]), wrow, wrow_t, A, A_t, "n1", ident_t)
        act_tiles = [T("act%d" % fc) for fc in range(44)]
        for es2 in [esG1]:
            ffn_up_phase(P, nc, es2, ps, C, A, A_t, dr, act_scr, act_tiles)
        esG1.close()
        P.barrier()
        x2_tiles = [[T("x2_%d_%d" % (tt, ft)) for ft in range(4)] for tt in range(16)]
        with ExitStack() as es2:
            ab = [sb(es2, nc, "ab%d" % i, [128, 44, 128], BF16) for i in range(2)]; ab_t = [T("ab%d" % i) for i in range(2)]
            xin = [sb(es2, nc, "p6xin%d" % i, [128, 512], F32) for i in range(3)]; xin_t = [T("xin%d" % i) for i in range(3)]
            xo = [sb(es2, nc, "p6xo%d" % i, [128, 512], F32) for i in range(3)]; xo_t = [T("xo%d" % i) for i in range(3)]
            cnt = [0]

            def A6(tt):
                i = cnt[0] % 2; cnt[0] += 1
                P.dma("sp", ab[i][:, :, :], act_scr[tt, :, :, :], act_tiles, [ab_t[i]])
                return (lambda kc: ab[i][:, kc, :]), [ab_t[i]]

            def pre6(tt, ft, it):
                i = it % 3
                P.dma("sp", xin[i][:, :], x1_scr[tt * 128:(tt + 1) * 128, ft * 512:(ft + 1) * 512], [x1_tiles[tt][ft]], [xin_t[i]])

            def epi6(tt, ft, pt, ptt, it):
                i = it % 3
                P.tt("dve", xo[i][:, :], pt, xin[i][:, :], ALU.add, [ptt, xin_t[i]], [xo_t[i]])
                P.dma("sp", x2_scr[tt * 128:(tt + 1) * 128, ft * 512:(ft + 1) * 512], xo[i][:, :], [xo_t[i]], [x2_tiles[tt][ft]])
            linear_tm(P, nc, es2, ps, "l6", A6, None, dr["ffn_w_down"], 44, 512, 4, epi6, extra=pre6)
        P.barrier()
        P.dma("sp", wrow[:, :], dr["ple_norm_w"].broadcast_to([128, 2048]), [], [wrow_t])
        esG2 = ExitStack()
        for es2 in [esG2]:
            norm_to_T(P, nc, es2, ps01, ident, lambda tt: (x2_scr[tt * 128:(tt + 1) * 128, :], x2_tiles[tt]), wrow, wrow_t, A, A_t, "n2", ident_t)
        x3_tiles = [[T("x3_%d_%d" % (tt, ft)) for ft in range(4)] for tt in range(16)]
        for es2 in [esG2]:
            pT = sb(es2, nc, "pT", [128, 2, 2048], BF16); pT_t = [T("pT%d" % i) for i in range(16)]
            pin = [sb(es2, nc, "pin%d" % i, [128, 256], F32) for i in range(2)]; pin_t = [T("pin%d" % i) for i in range(2)]
            pbf = [sb(es2, nc, "pbf%d" % i, [128, 256], BF16) for i in range(2)]; pbf_t = [T("pbf%d" % i) for i in range(2)]
            q2b = q[2].bitcast(BF16); q3b = q[3].bitcast(BF16)
            for tt in range(16):
                i = tt % 2
                qb, qbt = (q2b, q_t[2]) if i == 0 else (q3b, q_t[3])
                P.dma("sp", pin[i][:, :], dr["p"][tt * 128:(tt + 1) * 128, :], [], [pin_t[i]])
                P.copy("dve", pbf[i][:, :], pin[i][:, :], [pin_t[i]], [pbf_t[i]])
                for kc in range(2):
                    P.tr(qb[:, kc * 128:(kc + 1) * 128], pbf[i][:, kc * 128:(kc + 1) * 128], ident[:, :], [pbf_t[i], ident_t], [qbt])
                P.copy("act", pT[:, :, tt * 128:(tt + 1) * 128], qb[:, 0:256].rearrange("p (a b) -> p a b", a=2), [qbt], [pT_t[tt]])
            wpp = sb(es2, nc, "wpp", [128, 2, 2048], BF16); wpp_t = T("wpp")
            P.dma("pool", wpp[:, :, :], dr["ple_w_proj"].rearrange("(kc p) n -> p kc n", p=128), [], [wpp_t])
            xin = [sb(es2, nc, "p7xin%d" % i, [128, 512], F32) for i in range(3)]; xin_t = [T("xin%d" % i) for i in range(3)]
            xo = [sb(es2, nc, "p7xo%d" % i, [128, 512], F32) for i in range(3)]; xo_t = [T("xo%d" % i) for i in range(3)]
            sgb = [sb(es2, nc, "sgb%d" % i, [128, 512], F32) for i in range(2)]; sgb_t = [T("sgb%d" % i) for i in range(2)]

            def pre7(tt, ft, it):
                i = it % 3
                P.dma("sp", xin[i][:, :], x2_scr[tt * 128:(tt + 1) * 128, ft * 512:(ft + 1) * 512], [x2_tiles[tt][ft]], [xin_t[i]])

            def epi7(tt, ft, pt, ptt, it):
                i = it % 3; j = it % 2
                pp, ppt = q[2 + j], q_t[2 + j]
                for kc in range(2):
                    P.mm(pp[:, :], pT[:, kc, tt * 128:(tt + 1) * 128], wpp[:, kc, ft * 512:(ft + 1) * 512], kc == 0, kc == 1,
                         [pT_t[tt], wpp_t], [ppt])
                P.act(sgb[j][:, :], pt, AF.Sigmoid, [ptt], [sgb_t[j]])
                P.tt("dve", sgb[j][:, :], sgb[j][:, :], pp[:, :], ALU.mult, [sgb_t[j], ppt], [sgb_t[j]])
                P.tt("dve", xo[i][:, :], sgb[j][:, :], xin[i][:, :], ALU.add, [sgb_t[j], xin_t[i]], [xo_t[i]])
                P.dma("sp", x3_scr[tt * 128:(tt + 1) * 128, ft * 512:(ft + 1) * 512], xo[i][:, :], [xo_t[i]], [x3_tiles[tt][ft]])
            linear_tm(P, nc, es2, ps, "l7", lambda tt: ((lambda kc: A[:, kc, tt * 128:(tt + 1) * 128]), [A_t[tt]]), None,
                      dr["ple_w_gate"], 16, 512, 4, epi7, extra=pre7)
        P.dma("sp", wrow[:, :], dr["final_norm_w"].broadcast_to([128, 2048]), [], [wrow_t])
        out_tiles = [T("out%d" % i) for i in range(16)]
        for es2 in [esG2]:
            final_norm(P, nc, es2, lambda tt: (x3_scr[tt * 128:(tt + 1) * 128, :], x3_tiles[tt]), wrow, wrow_t, out, out_tiles, "fn")
        esG2.close()
        P.op("sp", None, out_tiles, [])
        P.emit()
    return nc


def _prep_shared(inp):
    f = lambda a: np.ascontiguousarray(np.asarray(a, dtype=np.float32))
    sh = {}
    sh["w_in"] = f(inp["w_in"][0])
    for nm in ("norm_mix_w", "norm_ffn_w", "ple_norm_w"):
        sh[nm] = f(inp[nm][0][None, :])
    sh["final_norm_w"] = f(inp["final_norm_w"][None, :])
    sh["ssm_cw"] = f(inp["ssm_conv_w"][0].T.reshape(48, 128, 4).transpose(1, 0, 2).reshape(128, 192))
    sh["ssm_cb"] = f(inp["ssm_conv_b"][0].reshape(48, 128).T)
    for nm in ("ssm_dt_bias", "ssm_a_log", "ssm_d", "ssm_norm_w"):
        sh[nm] = f(inp[nm][0][None, :])
    for nm in ("cmp_wk1", "cmp_wv1", "cmp_wk2", "cmp_wv2", "w_ssm_branch", "w_attn_branch", "w_mix_out",
               "ffn_w_gate", "ffn_w_up", "ffn_w_down", "ple_w_gate", "ple_w_proj"):
        sh[nm] = f(inp[nm][0])
    sh["cmp_pekT"] = f(inp["cmp_pe_k"][0].T)
    sh["cmp_pevT"] = f(inp["cmp_pe_v"][0].T)
    sh["ffn_cw"] = f(inp["ffn_conv_w"][0].T.reshape(44, 128, 3).transpose(1, 0, 2).reshape(128, 132))
    sh["ffn_cb"] = f(inp["ffn_conv_b"][0].reshape(44, 128).T)
    sh.update(consts_np())
    return sh


def kernel(**inp):
    nc = build_program()
    sh = _prep_shared(inp)
    x = np.asarray(inp["x"], dtype=np.float32)
    p = np.asarray(inp["p"], dtype=np.float32)
    in_maps = []
    for b in range(8):
        m = dict(sh)
        m["x"] = np.ascontiguousarray(x[b])
        m["p"] = np.ascontiguousarray(p[0, b])
        in_maps.append(m)
    res = run_bass_kernel_spmd(nc, in_maps, core_ids=list(range(8)))
    return np.stack([np.asarray(res.results[b]["out"], dtype=np.float32) for b in range(8)], axis=0)
```
